# Optimizing a Trainium2 kernel written in Bass

```python
import math
import jax
import jax.numpy as jnp
from jax import lax
import numpy as np

D_MODEL = 1024
BATCH = 8
SEQ = 8192
DEPTH = 1
DEC_BATCH = 8
DEC_SEQ = 32
PAST_LEN = 2048

CHUNK = 64
D_MIX = D_MODEL
SSD_WIDTH = D_MIX // 2
SSD_HEAD_DIM = 64
SSD_HEADS = SSD_WIDTH // SSD_HEAD_DIM
SSD_GROUPS = 2
SSD_HEADS_PER_GROUP = SSD_HEADS // SSD_GROUPS
SSD_STATE = 128
SSD_CONV = 4
SSD_CONV_DIM = SSD_WIDTH + 2 * SSD_GROUPS * SSD_STATE
SSD_CHUNK = CHUNK
S5_WIDTH = D_MIX - SSD_WIDTH
S5_GROUP_CH = 16
S5_GROUPS = S5_WIDTH // S5_GROUP_CH
S5_STATE = 64
D_FF = 2816
FFN_CONV = 3
D_IN = SSD_WIDTH + SSD_CONV_DIM + SSD_HEADS + S5_WIDTH
EPS = 1e-6

kernel_name = "hybrid_ssd_s5_streaming_step"


def rmsnorm(x, w):
    xf = x.astype(jnp.float32)
    xf = xf * lax.rsqrt(jnp.mean(xf * xf, axis=-1, keepdims=True) + EPS)
    return (xf * w.astype(jnp.float32)).astype(x.dtype)


def causal_dwconv(u, hist, w, b):
    k = w.shape[0]
    length = u.shape[1]
    full = jnp.concatenate([hist.astype(u.dtype), u], axis=1)
    out = b + full[:, 0:length] * w[0]
    for i in range(1, k):
        out = out + full[:, i:i + length] * w[i]
    return out, full[:, length:]


def ssd_scan(xs, dt, a, bm, cm, h0):
    bsz, length, nh, hd = xs.shape
    q = SSD_CHUNK if length % SSD_CHUNK == 0 else length
    nc = length // q
    x_c = xs.reshape(bsz, nc, q, nh, hd)
    dt_c = dt.reshape(bsz, nc, q, nh)
    b_c = bm.reshape(bsz, nc, q, nh, SSD_STATE)
    c_c = cm.reshape(bsz, nc, q, nh, SSD_STATE)
    cs = jnp.cumsum(dt_c * a, axis=2)
    causal = jnp.tril(jnp.ones((q, q), dtype=bool))[None, None, :, :, None]
    seg = cs[:, :, :, None, :] - cs[:, :, None, :, :]
    decay = jnp.exp(jnp.where(causal, seg, -jnp.inf))
    scores = jnp.einsum("bclhn,bcshn->bclsh", c_c, b_c)
    y_diag = jnp.einsum("bclsh,bcshp->bclhp", scores * decay * dt_c[:, :, None, :, :], x_c)
    w_end = jnp.exp(cs[:, :, -1:, :] - cs) * dt_c
    states = jnp.einsum("bcshn,bcsh,bcshp->bchpn", b_c, w_end, x_c)
    chunk_decay = jnp.exp(cs[:, :, -1, :])

    def step(h, inp):
        st, dcy = inp
        return h * dcy[:, :, None, None] + st, h

    h_final, h_prev = lax.scan(step, h0.astype(states.dtype),
                               (jnp.moveaxis(states, 1, 0), jnp.moveaxis(chunk_decay, 1, 0)))
    h_prev = jnp.moveaxis(h_prev, 0, 1)
    y_off = jnp.einsum("bclhn,bchpn,bclh->bclhp", c_c, h_prev, jnp.exp(cs))
    return (y_diag + y_off).reshape(bsz, length, nh, hd), h_final


def ssd_mixer(z, xbc, dt_raw, conv_hist, h0, conv_w, conv_b, dt_bias, a_log, d, norm_w):
    bsz, length, _ = z.shape
    xbc, new_conv = causal_dwconv(xbc, conv_hist, conv_w, conv_b)
    xbc = jax.nn.silu(xbc)
    xs, bm, cm = jnp.split(xbc, [SSD_WIDTH, SSD_WIDTH + SSD_GROUPS * SSD_STATE], axis=-1)
    xs = xs.reshape(bsz, length, SSD_HEADS, SSD_HEAD_DIM)
    bm = jnp.repeat(bm.reshape(bsz, length, SSD_GROUPS, SSD_STATE), SSD_HEADS_PER_GROUP, axis=2)
    cm = jnp.repeat(cm.reshape(bsz, length, SSD_GROUPS, SSD_STATE), SSD_HEADS_PER_GROUP, axis=2)
    dt = jax.nn.softplus(dt_raw + dt_bias)
    a = -jnp.exp(a_log)
    y, h_new = ssd_scan(xs, dt, a, bm, cm, h0)
    y = y + d[:, None] * xs
    g = (y.reshape(bsz, length, SSD_WIDTH) * jax.nn.silu(z)).reshape(bsz, length, SSD_GROUPS, -1)
    g = rmsnorm(g, norm_w.reshape(SSD_GROUPS, -1)).reshape(bsz, length, SSD_WIDTH)
    return g, new_conv, h_new


def complex_affine_combine(e1, e2):
    a1r, a1i, b1r, b1i = e1
    a2r, a2i, b2r, b2i = e2
    return (a2r * a1r - a2i * a1i,
            a2r * a1i + a2i * a1r,
            a2r * b1r - a2i * b1i + b2r,
            a2r * b1i + a2i * b1r + b2i)


def s5_mixer(u, h0_re, h0_im, lam_re, lam_im, log_dt, b_re, b_im, c_re, c_im, d, glu_w, glu_b):
    bsz, length, _ = u.shape
    ug = u.reshape(bsz, length, S5_GROUPS, S5_GROUP_CH)
    dt = jnp.exp(log_dt)[:, None]
    mag = jnp.exp(lam_re * dt)
    ang = lam_im * dt
    lb_re = mag * jnp.cos(ang)
    lb_im = mag * jnp.sin(ang)
    den = lam_re * lam_re + lam_im * lam_im
    q_re = ((lb_re - 1) * lam_re + lb_im * lam_im) / den
    q_im = (lb_im * lam_re - (lb_re - 1) * lam_im) / den
    bb_re = q_re[..., None] * b_re - q_im[..., None] * b_im
    bb_im = q_re[..., None] * b_im + q_im[..., None] * b_re
    bu_re = jnp.einsum("blgc,gpc->blgp", ug, bb_re)
    bu_im = jnp.einsum("blgc,gpc->blgp", ug, bb_im)
    first_re = bu_re[:, 0] + lb_re * h0_re - lb_im * h0_im
    first_im = bu_im[:, 0] + lb_re * h0_im + lb_im * h0_re
    bu_re = bu_re.at[:, 0].set(first_re.astype(bu_re.dtype))
    bu_im = bu_im.at[:, 0].set(first_im.astype(bu_im.dtype))
    a_re = jnp.broadcast_to(lb_re, (1, length, S5_GROUPS, S5_STATE))
    a_im = jnp.broadcast_to(lb_im, (1, length, S5_GROUPS, S5_STATE))
    _, _, h_re, h_im = lax.associative_scan(complex_affine_combine, (a_re, a_im, bu_re, bu_im), axis=1)
    y = (jnp.einsum("blgp,gcp->blgc", h_re, c_re) - jnp.einsum("blgp,gcp->blgc", h_im, c_im)
         + d * ug)
    g = jnp.einsum("blgc,gck->blgk", jax.nn.gelu(y, approximate=True), glu_w) + glu_b
    out = g[..., :S5_GROUP_CH] * jax.nn.sigmoid(g[..., S5_GROUP_CH:])
    return out.reshape(bsz, length, S5_WIDTH), h_re[:, -1], h_im[:, -1]


def hybrid_layer(x, conv_hist, ssd_h0, s5_h0_re, s5_h0_im, ffn_hist,
                 pre_mix_norm_w, w_in, ssd_conv_w, ssd_conv_b, ssd_dt_bias, ssd_a_log, ssd_d, ssd_norm_w,
                 s5_lambda_re, s5_lambda_im, s5_log_dt, s5_b_re, s5_b_im, s5_c_re, s5_c_im, s5_d,
                 s5_glu_w, s5_glu_b, w_out, post_mix_norm_w, pre_ffn_norm_w, w_up, ffn_conv_w, ffn_conv_b,
                 w_down, post_ffn_norm_w):
    xn = rmsnorm(x, pre_mix_norm_w)
    proj = xn @ w_in
    z, xbc, dt_raw, u_s5 = jnp.split(
        proj, [SSD_WIDTH, SSD_WIDTH + SSD_CONV_DIM, SSD_WIDTH + SSD_CONV_DIM + SSD_HEADS], axis=-1)
    y_ssd, new_conv, new_ssd = ssd_mixer(z, xbc, dt_raw, conv_hist, ssd_h0, ssd_conv_w, ssd_conv_b,
                                         ssd_dt_bias, ssd_a_log, ssd_d, ssd_norm_w)
    y_s5, new_re, new_im = s5_mixer(u_s5, s5_h0_re, s5_h0_im, s5_lambda_re, s5_lambda_im, s5_log_dt,
                                    s5_b_re, s5_b_im, s5_c_re, s5_c_im, s5_d, s5_glu_w, s5_glu_b)
    mix = jnp.concatenate([y_ssd, y_s5], axis=-1) @ w_out
    h = x + rmsnorm(mix, post_mix_norm_w)
    up, new_ffn = causal_dwconv(rmsnorm(h, pre_ffn_norm_w) @ w_up, ffn_hist, ffn_conv_w, ffn_conv_b)
    gate, val = jnp.split(up, 2, axis=-1)
    ffn = (jax.nn.gelu(gate, approximate=True) * val) @ w_down
    y = h + rmsnorm(ffn, post_ffn_norm_w)
    return y, new_conv, new_ssd, new_re, new_im, new_ffn


def setup_inputs(seed: int = 0) -> dict:
    key = jax.random.key(seed)
    ks = iter(jax.random.split(key, 48))

    def nrm(shape, scale):
        return jax.random.normal(next(ks), shape, jnp.float32) * scale

    def gain(shape):
        return 1.0 + nrm(shape, 0.05)

    dt0 = jnp.exp(jax.random.uniform(next(ks), (DEPTH, SSD_HEADS), jnp.float32,
                                     minval=math.log(1e-3), maxval=math.log(1e-1)))
    n_idx = jnp.arange(S5_STATE, dtype=jnp.float32)
    return {
        "x_prompt": nrm((BATCH, SEQ, D_MODEL), 1.0),
        "x_sample": nrm((DEC_BATCH, DEC_SEQ, D_MODEL), 1.0),
        "cache_ssd_conv": nrm((DEPTH, DEC_BATCH, SSD_CONV - 1, SSD_CONV_DIM), 1.0),
        "state_ssd": nrm((DEPTH, DEC_BATCH, SSD_HEADS, SSD_HEAD_DIM, SSD_STATE), 0.1),
        "state_s5_re": nrm((DEPTH, DEC_BATCH, S5_GROUPS, S5_STATE), 0.5),
        "state_s5_im": nrm((DEPTH, DEC_BATCH, S5_GROUPS, S5_STATE), 0.5),
        "cache_ffn_conv": nrm((DEPTH, DEC_BATCH, FFN_CONV - 1, 2 * D_FF), 1.0),
        "pre_mix_norm_w": gain((DEPTH, D_MODEL)),
        "w_in": nrm((DEPTH, D_MODEL, D_IN), D_MODEL ** -0.5),
        "ssd_conv_w": nrm((DEPTH, SSD_CONV, SSD_CONV_DIM), 0.3),
        "ssd_conv_b": nrm((DEPTH, SSD_CONV_DIM), 0.02),
        "ssd_dt_bias": dt0 + jnp.log(-jnp.expm1(-dt0)),
        "ssd_a_log": jnp.log(jax.random.uniform(next(ks), (DEPTH, SSD_HEADS), jnp.float32, minval=1.0, maxval=16.0)),
        "ssd_d": 1.0 + nrm((DEPTH, SSD_HEADS), 0.1),
        "ssd_norm_w": gain((DEPTH, SSD_WIDTH)),
        "s5_lambda_re": -0.5 + nrm((DEPTH, S5_GROUPS, S5_STATE), 0.01),
        "s5_lambda_im": math.pi * n_idx + nrm((DEPTH, S5_GROUPS, S5_STATE), 0.01),
        "s5_log_dt": jax.random.uniform(next(ks), (DEPTH, S5_GROUPS), jnp.float32,
                                        minval=math.log(1e-3), maxval=math.log(1e-1)),
        "s5_b_re": nrm((DEPTH, S5_GROUPS, S5_STATE, S5_GROUP_CH), (2 * S5_GROUP_CH) ** -0.5),
        "s5_b_im": nrm((DEPTH, S5_GROUPS, S5_STATE, S5_GROUP_CH), (2 * S5_GROUP_CH) ** -0.5),
        "s5_c_re": nrm((DEPTH, S5_GROUPS, S5_GROUP_CH, S5_STATE), (2 * S5_STATE) ** -0.5),
        "s5_c_im": nrm((DEPTH, S5_GROUPS, S5_GROUP_CH, S5_STATE), (2 * S5_STATE) ** -0.5),
        "s5_d": nrm((DEPTH, S5_GROUPS, S5_GROUP_CH), 0.5),
        "s5_glu_w": nrm((DEPTH, S5_GROUPS, S5_GROUP_CH, 2 * S5_GROUP_CH), S5_GROUP_CH ** -0.5),
        "s5_glu_b": nrm((DEPTH, S5_GROUPS, 2 * S5_GROUP_CH), 0.02),
        "w_out": nrm((DEPTH, D_MIX, D_MODEL), D_MIX ** -0.5),
        "post_mix_norm_w": gain((DEPTH, D_MODEL)),
        "pre_ffn_norm_w": gain((DEPTH, D_MODEL)),
        "w_up": nrm((DEPTH, D_MODEL, 2 * D_FF), D_MODEL ** -0.5),
        "ffn_conv_w": nrm((DEPTH, FFN_CONV, 2 * D_FF), 0.5),
        "ffn_conv_b": nrm((DEPTH, 2 * D_FF), 0.02),
        "w_down": nrm((DEPTH, D_FF, D_MODEL), D_FF ** -0.5),
        "post_ffn_norm_w": gain((DEPTH, D_MODEL)),
    }


def reference(x_prompt, x_sample, cache_ssd_conv, state_ssd, state_s5_re, state_s5_im, cache_ffn_conv,
              pre_mix_norm_w, w_in, ssd_conv_w, ssd_conv_b, ssd_dt_bias, ssd_a_log, ssd_d, ssd_norm_w,
              s5_lambda_re, s5_lambda_im, s5_log_dt, s5_b_re, s5_b_im, s5_c_re, s5_c_im, s5_d,
              s5_glu_w, s5_glu_b, w_out, post_mix_norm_w, pre_ffn_norm_w, w_up, ffn_conv_w, ffn_conv_b,
              w_down, post_ffn_norm_w):
    bsz = x_prompt.shape[0]
    dtp = x_prompt.dtype
    y_prompt, y_sample = x_prompt, x_sample
    p_conv, p_ssd, p_re, p_im, p_ffn = [], [], [], [], []
    s_conv, s_ssd, s_re, s_im, s_ffn = [], [], [], [], []
    for l in range(DEPTH):
        lw = (pre_mix_norm_w[l], w_in[l], ssd_conv_w[l], ssd_conv_b[l], ssd_dt_bias[l], ssd_a_log[l],
              ssd_d[l], ssd_norm_w[l], s5_lambda_re[l], s5_lambda_im[l], s5_log_dt[l], s5_b_re[l],
              s5_b_im[l], s5_c_re[l], s5_c_im[l], s5_d[l], s5_glu_w[l], s5_glu_b[l], w_out[l],
              post_mix_norm_w[l], pre_ffn_norm_w[l], w_up[l], ffn_conv_w[l], ffn_conv_b[l], w_down[l],
              post_ffn_norm_w[l])
        y_prompt, pc, ph, pr, pi_, pf = hybrid_layer(
            y_prompt,
            jnp.zeros((bsz, SSD_CONV - 1, SSD_CONV_DIM), dtp),
            jnp.zeros((bsz, SSD_HEADS, SSD_HEAD_DIM, SSD_STATE), dtp),
            jnp.zeros((bsz, S5_GROUPS, S5_STATE), dtp),
            jnp.zeros((bsz, S5_GROUPS, S5_STATE), dtp),
            jnp.zeros((bsz, FFN_CONV - 1, 2 * D_FF), dtp),
            *lw)
        y_sample, sc, sh, sr, si, sf = hybrid_layer(
            y_sample, cache_ssd_conv[l], state_ssd[l], state_s5_re[l], state_s5_im[l], cache_ffn_conv[l], *lw)
        p_conv.append(pc); p_ssd.append(ph); p_re.append(pr); p_im.append(pi_); p_ffn.append(pf)
        s_conv.append(sc); s_ssd.append(sh); s_re.append(sr); s_im.append(si); s_ffn.append(sf)
    new_ssd_conv_prompt = jnp.stack(p_conv)
    new_ssd_state_prompt = jnp.stack(p_ssd)
    new_s5_re_prompt = jnp.stack(p_re)
    new_s5_im_prompt = jnp.stack(p_im)
    new_ffn_conv_prompt = jnp.stack(p_ffn)
    new_ssd_conv_sample = jnp.stack(s_conv)
    new_ssd_state_sample = jnp.stack(s_ssd)
    new_s5_re_sample = jnp.stack(s_re)
    new_s5_im_sample = jnp.stack(s_im)
    new_ffn_conv_sample = jnp.stack(s_ffn)
    return (y_prompt, y_sample,
            new_ssd_conv_prompt, new_ssd_state_prompt, new_s5_re_prompt, new_s5_im_prompt, new_ffn_conv_prompt,
            new_ssd_conv_sample, new_ssd_state_sample, new_s5_re_sample, new_s5_im_sample, new_ffn_conv_sample)
```

```python
import numpy as np
import concourse.bass as bass
import concourse.mybir as mybir
from concourse.bass_utils import run_bass_kernel_spmd

F32 = mybir.dt.float32
BF16 = mybir.dt.bfloat16
AF = mybir.ActivationFunctionType
ALU = mybir.AluOpType

ENGS = ("pe", "act", "dve", "pool", "sp")
NO_SELF_SYNC = ("pe",)
NCORES = 8
T_PROMPT = 8192
T_SAMPLE = 32
NT = 512
EPS = 1e-6
NWIN = 9
NWUP = 22
RING = 5
STAGE = 9
SUB = 9
DBG = 0


class Buf:
    __slots__ = ("name", "w", "r", "al")

    def __init__(self, name):
        self.name = name
        self.w = None
        self.r = {}
        self.al = []


class Prog:
    def __init__(self, nc):
        self.nc = nc
        self.ops = {e: [] for e in ENGS}
        self.cnt = {}
        self.seen = {e: {} for e in ENGS}
        self.sems = {}

    def sem(self, key):
        if key not in self.sems:
            self.sems[key] = self.nc.alloc_semaphore("s_" + str(key).replace(" ", ""))
            self.cnt[key] = 0
        return self.sems[key]

    def op(self, eng, fn, reads=(), writes=(), sig=True, dma=None):
        waits = {}

        def need(ev):
            if ev is None:
                return
            k, v = ev
            if k == eng and eng in NO_SELF_SYNC:
                return
            if v > waits.get(k, 0):
                waits[k] = v
        for b in reads:
            need(b.w)
        for b in writes:
            for bb in [b] + b.al:
                if not (bb.w is not None and bb.w[0] == eng and eng in ("act", "dve")):
                    need(bb.w)
                for k, v in bb.r.items():
                    if k == eng and eng in ("act", "dve"):
                        continue
                    need((k, v))
        wl = []
        for k, v in waits.items():
            if self.seen[eng].get(k, 0) >= v:
                continue
            self.seen[eng][k] = v
            wl.append((k, v))
        if dma is not None:
            key = dma
            self.sem(key)
            self.cnt[key] += 16
            val = self.cnt[key]
            mode = ("dma", key)
        else:
            key = eng
            self.sem(key)
            if sig:
                self.cnt[key] += 1
                val = self.cnt[key]
                mode = ("sig", key)
            else:
                val = self.cnt[key] + 1
                mode = None
        self.ops[eng].append((wl, fn, mode))
        for b in reads:
            if b.r.get(key, 0) < val:
                b.r[key] = val
        for b in writes:
            b.w = (key, val)
            b.r = {}

    def emit(self):
        nc = self.nc
        with nc.Block() as block:
            def mk(ename):
                def body(e):
                    for wl, fn, mode in self.ops[ename]:
                        for k, v in wl:
                            e.wait_ge(self.sems[k], v)
                        ins = fn(e)
                        if mode is not None:
                            kind, key = mode
                            ins.then_inc(self.sems[key], 16 if kind == "dma" else 1)
                return body
            block.tensor(mk("pe"))
            block.scalar(mk("act"))
            block.vector(mk("dve"))
            block.gpsimd(mk("pool"))
            block.sync(mk("sp"))


class Packer:
    def __init__(self):
        self.items = []
        self.off = {}
        self.n = 0

    def add(self, name, arr):
        arr = np.asarray(arr, np.float32)
        if arr.shape[0] < 128:
            pad = np.zeros((128,) + arr.shape[1:], np.float32)
            pad[:arr.shape[0]] = arr
            arr = pad
        a2 = arr.reshape(128, -1)
        self.off[name] = (self.n, a2.shape[1], arr.shape[1:])
        self.items.append(a2)
        self.n += a2.shape[1]

    def build(self):
        return np.ascontiguousarray(np.concatenate(self.items, 1))


def pg(a):
    a = np.asarray(a, np.float32)
    sh = a.shape[2:]
    return np.ascontiguousarray(
        a.reshape(16, 2, 64, *sh).transpose(1, 2, 0, *range(3, 3 + len(sh))).reshape(128, 16, *sh))


def unpg(x):
    return np.asarray(x).reshape(2, 64, 16).transpose(2, 0, 1).reshape(32, 64)


def chunkify(W):
    return np.ascontiguousarray(W.reshape(8, 128, 256).transpose(1, 0, 2)).reshape(128, 2048)


FFN_COLS = np.concatenate([np.concatenate([np.arange(m * 128, (m + 1) * 128),
                                           np.arange(2816 + m * 128, 2816 + (m + 1) * 128)]) for m in range(22)])


def host_shared(inp):
    g = lambda k: np.asarray(inp[k], np.float32)[0]
    W_in = g("w_in")
    win = []
    for c0 in (0, 256):
        win.append(chunkify(W_in[:, c0:c0 + 256]))
    for c0 in range(512, 1536, 256):
        win.append(chunkify(W_in[:, c0:c0 + 256]))
    dtp = np.zeros((1024, 256), np.float32)
    dtp[:, :8] = W_in[:, 1536:1544]
    win.append(chunkify(dtp))
    for c0 in (1544, 1800):
        win.append(chunkify(W_in[:, c0:c0 + 256]))
    W_up = g("w_up")
    for m in range(22):
        win.append(chunkify(W_up[:, FFN_COLS[m * 256:(m + 1) * 256]]))
    wstream = np.ascontiguousarray(np.stack(win))
    W_down = g("w_down")
    wd = np.ascontiguousarray(W_down.reshape(11, 2, 128, 1024).transpose(0, 2, 1, 3).reshape(11, 128, 2048))
    wout = np.ascontiguousarray(g("w_out").reshape(8, 128, 1024).transpose(1, 0, 2).reshape(128, 8192))

    pk = Packer()
    pk.add("ident", np.eye(128))
    s_i = np.arange(128)
    pk.add("negmask", np.where(s_i[None, :] >= s_i[:, None], 0.0, -1e30))
    km = (np.arange(8)[None, :] >= np.arange(8)[:, None]).astype(np.float32)
    pk2 = Packer()
    pk2.add("kmask", np.kron(km, np.ones((16, 16))))
    pk2.add("rm", np.stack([(np.arange(128) < 64), (np.arange(128) >= 64)], 1).astype(np.float32))
    kvec = lambda v, nk: v.reshape(nk, 128).T
    pk.add("wpre_k", kvec(g("pre_mix_norm_w"), 8))
    pk.add("wffn_k", kvec(g("pre_ffn_norm_w"), 8))
    pk.add("wssd_k", kvec(g("ssd_norm_w"), 4))
    pk.add("wpost_bc", np.tile(g("post_mix_norm_w")[None, :], (128, 1)))
    pk.add("wpf_bc", np.tile(g("post_ffn_norm_w")[None, :], (128, 1)))
    pk.add("cw", g("ssd_conv_w").reshape(4, 8, 128).transpose(2, 1, 0))
    pk.add("cb", g("ssd_conv_b").reshape(8, 128).T)
    pk.add("fw", g("ffn_conv_w")[:, FFN_COLS].reshape(3, 44, 128).transpose(2, 1, 0))
    pk.add("fb", g("ffn_conv_b")[FFN_COLS].reshape(44, 128).T)
    pk.add("D_bc", np.tile(g("ssd_d")[None, :], (128, 1)))
    selh = np.zeros((8, 8, 128), np.float32)
    for h in range(8):
        selh[h, h, :] = 1.0
    pk.add("selh", selh)
    pk.add("dtb8", g("ssd_dt_bias").reshape(8, 1))
    pk.add("alog8", g("ssd_a_log").reshape(8, 1))
    pk2.add("lamre", pg(g("s5_lambda_re")))
    pk2.add("lamim", pg(g("s5_lambda_im")))
    pk2.add("logdt", pg(np.repeat(g("s5_log_dt")[:, None], 64, 1)))
    pk2.add("Bre", pg(g("s5_b_re")))
    pk2.add("Bim", pg(g("s5_b_im")))
    pk2.add("Cre", pg(g("s5_c_re").transpose(0, 2, 1)))
    pk2.add("Cim", pg(g("s5_c_im").transpose(0, 2, 1)))
    pk.add("d8", np.tile(g("s5_d"), (1, 8)).T)
    gw = g("s5_glu_w")
    gb = g("s5_glu_b")
    GW = np.zeros((128, 4, 2, 128), np.float32)
    for gg in range(32):
        blk, g8 = gg // 8, gg % 8
        GW[g8 * 16:(g8 + 1) * 16, blk, 0, g8 * 16:(g8 + 1) * 16] = gw[gg][:, :16]
        GW[g8 * 16:(g8 + 1) * 16, blk, 1, g8 * 16:(g8 + 1) * 16] = gw[gg][:, 16:]
    pk2.add("GW", GW)
    pk.add("gba", gb[:, :16].reshape(4, 128).T)
    pk.add("gbb", gb[:, 16:].reshape(4, 128).T)
    return dict(wstream=wstream, wd=wd, wout=wout, cst=pk.build(), cst2=pk2.build()), pk, pk2


def host_core(inp, i, T):
    f = lambda a: np.ascontiguousarray(np.asarray(a, np.float32))
    d = {}
    d["x"] = f(inp["x_prompt"][i][:T])
    d["xs"] = f(inp["x_sample"][i])
    d["c_conv"] = f(np.asarray(inp["cache_ssd_conv"])[0, i].reshape(3, 8, 128).transpose(2, 1, 0))
    d["c_ssd"] = f(np.asarray(inp["state_ssd"])[0, i].reshape(4, 128, 128).transpose(1, 0, 2))
    d["c_s5re"] = f(pg(np.asarray(inp["state_s5_re"])[0, i]))
    d["c_s5im"] = f(pg(np.asarray(inp["state_s5_im"])[0, i]))
    d["c_ffn"] = f(np.asarray(inp["cache_ffn_conv"])[0, i][:, FFN_COLS].reshape(2, 44, 128).transpose(2, 1, 0))
    return d


def build(T, pk, pk2):
    nc = bass.Bass("TRN2", target_bir_lowering=False)
    P = Prog(nc)
    NCST = pk.n

    def din(name, shape, dt=F32):
        return nc.dram_tensor(name, list(shape), dt, kind="ExternalInput").ap()

    def dout(name, shape):
        return nc.dram_tensor(name, list(shape), F32, kind="ExternalOutput").ap()
    x_d = din("x", [T, 1024])
    xs_d = din("xs", [T_SAMPLE, 1024])
    cconv_d = din("c_conv", [128, 8, 3])
    cssd_d = din("c_ssd", [128, 4, 128])
    cs5re_d = din("c_s5re", [128, 16])
    cs5im_d = din("c_s5im", [128, 16])
    cffn_d = din("c_ffn", [128, 44, 2])
    wstream_d = din("wstream", [NWIN + NWUP, 128, 2048])
    wd_d = din("wd", [11, 128, 2048])
    wout_d = din("wout", [128, 8192])
    cst_d = din("cst", [128, NCST])
    cst2_d = din("cst2", [128, pk2.n])
    y_d = dout("y", [T, 1024])
    ys_d = dout("ys", [T_SAMPLE, 1024])
    outs = {}
    for sfx in ("p", "s"):
        outs["conv" + sfx] = dout("o_conv" + sfx, [128, 8, 3])
        outs["ssd" + sfx] = dout("o_ssd" + sfx, [128, 4, 128])
        outs["s5re" + sfx] = dout("o_s5re" + sfx, [128, 16])
        outs["s5im" + sfx] = dout("o_s5im" + sfx, [128, 16])
        outs["ffn" + sfx] = dout("o_ffn" + sfx, [128, 44, 2])
    wsc_d = nc.dram_tensor("wsc", [NWIN + NWUP, 128, 2048], BF16).ap()
    wdsc_d = nc.dram_tensor("wdsc", [11, 128, 2048], BF16).ap()
    woutsc_d = nc.dram_tensor("woutsc", [128, 8192], BF16).ap()

    def sb(name, shape, dt=F32):
        return nc.alloc_sbuf_tensor("sb_" + name, list(shape), dt)

    B = {}

    def bf(n):
        if n not in B:
            B[n] = Buf(n)
        return B[n]

    ARENA_BYTES = 56 * 1024
    arena = sb("arena", [128, ARENA_BYTES // 4])
    arena_addr = nc.lookup_mloc(arena).addr
    sect_off = {}
    arena_items = []

    ar_pos = {}

    def ar(section, name, shape, dt=F32, bufname=None, at=None):
        esz = 4 if dt == F32 else 2
        n = int(np.prod(shape[1:]))
        nb4 = (n * esz + 31) // 32 * 32
        if at is not None:
            off = ar_pos[at]
        else:
            off = sect_off.get(section, 0)
            sect_off[section] = off + nb4
        ar_pos[name] = off
        assert off + nb4 <= ARENA_BYTES, (section, name, off + nb4)
        ap = nc.alloc_sbuf_tensor_at("ar_%s_%s" % (section, name), list(shape), dt, offset=arena_addr + off)
        b = bf(bufname or name)
        for (sec2, o2, s2, b2) in arena_items:
            if sec2 != section and b2 is not b and not (off + nb4 <= o2 or o2 + s2 <= off):
                if b2 not in b.al:
                    b.al.append(b2)
                if b not in b2.al:
                    b2.al.append(b)
        arena_items.append((section, off, nb4, b))
        return ap

    cst = sb("cst", [128, NCST])

    def C(name):
        off, n, shp = pk.off[name]
        ap = cst[:, off:off + n]
        if len(shp) == 2:
            ap = ap.rearrange("p (a b) -> p a b", a=shp[0])
        elif len(shp) == 3:
            ap = ap.rearrange("p (a b c) -> p a b c", a=shp[0], b=shp[1])
        return ap
    pcst = ar("pro", "pcst", [128, pk2.n], bufname="s5c")

    def C2(name):
        off, n, shp = pk2.off[name]
        ap = pcst[:, off:off + n]
        if len(shp) == 2:
            ap = ap.rearrange("p (a b) -> p a b", a=shp[0])
        elif len(shp) == 3:
            ap = ap.rearrange("p (a b c) -> p a b c", a=shp[0], b=shp[1])
        return ap
    identF = C("ident")
    identB = sb("identB", [128, 128], BF16)
    DI = sb("DI", [128, 8, 128], BF16)
    a8 = sb("a8", [8, 1])
    cmask = sb("cmask", [8, NT])
    wout_sb = sb("wout_sb", [128, 8, 1024], BF16)
    GWb = sb("GWb", [128, 4, 2, 128], BF16)
    ZT = sb("ZT", [128, 2, 32, 128], BF16)
    Fg = sb("Fg", [128, 2, 32, 128], BF16)
    Km = sb("Km", [128, 32, 128], BF16)
    JMAX = NT // 8
    cosT = sb("cosT", [128, 16, JMAX])
    sinT = sb("sinT", [128, 16, JMAX])
    r8z = sb("r8z", [128, 16, JMAX])
    r8 = sb("r8", [128, 16])
    xh = sb("xh", [128, 8, 3])
    stT = sb("stT", [128, 512])
    stB = sb("stB", [128, 512], BF16)
    Hc = sb("Hc", [128, 2, 16])
    fh = sb("fh", [128, 44, 2])
    xres = sb("xres", [128, 4, 1024])
    xb16s = [sb("xb16_%d" % i, [128, 1024], BF16) for i in range(2)]
    junk = sb("junk", [128, 1024], BF16)
    stt = sb("stt", [128, 4, 8])
    xnT = sb("xnT", [128, 8, NT], BF16)
    mixinT = sb("mixinT", [128, 8, NT], BF16)
    tk = sb("tk", [128, 4, 16])
    ek = sb("ek", [128, 4, 8])
    wk = sb("wk", [128, 8])
    wk2s = [sb("wk2_%d" % i, [128, 8]) for i in range(2)]
    cks = [sb("ck%d" % i, [128, 8]) for i in range(2)]
    gs = sb("gs", [128, 8])
    ring = [sb("ring%d" % i, [128, 2048], BF16) for i in range(RING)]
    zs = ar("ssd", "zs", [128, 4, 512], BF16)
    ct = [ar("ssd", "ct%d" % i, [128, NT]) for i in range(2)]
    xbw = [ar("ssd", "xbw%d" % i, [128, 3 + NT]) for i in range(2)]
    xbcT = ar("ssd", "xbcT", [128, 8, NT], BF16)
    dtmp = ar("ssd", "dtmp", [8, NT])
    dtT = ar("ssd", "dtT", [8, NT])
    dtaT = ar("ssd", "dtaT", [8, NT])
    csT = ar("ssd", "csT", [8, NT])
    xsb_toks = [ar("ssd", "xsb_tok%d" % i, [128, 6, 128], BF16) for i in range(2)]
    arg = ar("ssd", "arg", [128, 8, 128])
    Eb = ar("ssd", "Eb", [128, 8, 128])
    Mbs = [ar("ssd", "Mb%d" % i, [128, 8, 128], BF16) for i in range(2)]
    ytmp = ar("ssd", "ytmp", [128, 512])
    ytmp2 = ar("ssd", "ytmp2", [128, 512])
    gbuf = ar("ssd", "gbuf", [128, 512])
    gn = ar("ssd", "gn", [128, 512], BF16)
    xw_tok = ar("ssd", "xw_tok", [128, 512], BF16)
    u_tok2 = ar("s5", "u_tok2", [64, 32, 8, 16], BF16)
    U8 = ar("s5", "U8", [128, 32, JMAX], BF16)
    R_ = ar("s5", "R_", [128, 2, 8, JMAX])
    t_a = ar("s5", "t_a", [128, 8, JMAX])
    t_b = ar("s5", "t_b", [128, 8, JMAX])
    G_ = ar("s5", "G_", [128, 2, 8, JMAX])
    Hn = ar("s5", "Hn", [128, 2, 16, JMAX])
    Hprev = ar("s5", "Hprev", [128, 2, 16, JMAX], BF16)
    yD8 = ar("s5", "yD8", [128, 32, JMAX], BF16)
    yj = ar("s5", "yj", [64, 8, 512], BF16)
    ygT = ar("s5", "ygT", [128, 4, NT], BF16)
    sgt = ar("s5", "sgt", [128, NT])
    actT = ar("ffn", "actT", [128, 22, NT], BF16)
    upb = [ar("ffn", "upb%d" % i, [128, 2 + NT]) for i in range(4)]
    fct = [ar("ffn", "fct%d" % i, [128, NT]) for i in range(4)]
    gels = [ar("ffn", "gel%d" % i, [128, NT]) for i in range(2)]
    mixtmps = [ar("ffn", "mixtmp%d" % i, [128, 1024]) for i in range(2)]
    mixtmp = mixtmps[0]
    NB = 6
    pb = [nc.alloc_psum_tensor("pb%d" % i, [128, 512], F32) for i in range(NB)]
    pt = [nc.alloc_psum_tensor("pt%d" % i, [128, 512], F32) for i in range(2)]
    b_pb = [Buf("pb%d" % i) for i in range(NB)]
    b_pt = [Buf("pt%d" % i) for i in range(2)]
    bank_i = [0, 0]

    def nbank():
        i = bank_i[0] % NB
        bank_i[0] += 1
        return pb[i], b_pb[i]

    def ntbank():
        i = bank_i[1] % 2
        bank_i[1] += 1
        return pt[i], b_pt[i]

    PRO = 9
    lvl = [0]

    def pe_transposes(n, src, K, M, src_reads, evac):
        for s0 in range(0, n, 4):
            cnt = min(4, n - s0)
            tb, tbb = ntbank()
            for i in range(cnt):
                op(PE, lambda e, tb=tb, i=i, s_=s0 + i: e.matmul(tb[:M, i * 128:i * 128 + K], lhsT=src(s_), rhs=identB[:K, :K], start=True, stop=True),
                   reads=list(src_reads) + [bf("identB")], writes=[tbb], sig=(i == cnt - 1))
            view = tb[:M, 0:cnt * 128].rearrange("p (a b) -> p a b", a=cnt)[:, :, :K]
            evac(s0, cnt, view, tbb)

    def op(eng, fn, **kw):
        if lvl[0] > PRO:
            return
        P.op(eng, fn, **kw)
    ACT, DVE, PE, POOL, SP = "act", "dve", "pe", "pool", "sp"

    S5 = bf("s5c")
    op(SP, lambda e: e.dma_start(out=cst[:], in_=cst_d), writes=[bf("cst")], dma="ld_cst")
    op(SP, lambda e: e.dma_start(out=pcst[:], in_=cst2_d), writes=[S5], dma="ld_cst2")
    for c in range(NWIN + NWUP):
        op(POOL, lambda e, c=c: e.dma_start(out=wsc_d[c], in_=wstream_d[c], max_dma_last_dim=8192),
           writes=[bf("wsc")], dma="castw_sc")
    for c in range(11):
        op(POOL, lambda e, c=c: e.dma_start(out=wdsc_d[c], in_=wd_d[c], max_dma_last_dim=8192),
           writes=[bf("wdsc")], dma="castw_wd")
    for c in range(4):
        op(POOL, lambda e, c=c: e.dma_start(out=woutsc_d[:, c * 2048:(c + 1) * 2048], in_=wout_d[:, c * 2048:(c + 1) * 2048],
                                            max_dma_last_dim=8192), writes=[bf("woutsc")], dma="castw_out")
    op(SP, lambda e: e.dma_start(out=wout_sb[:].rearrange("p k c -> p (k c)"), in_=woutsc_d), reads=[bf("woutsc")],
       writes=[bf("wout_sb")], dma="ld_wout")
    op(DVE, lambda e: e.tensor_copy(out=identB[:], in_=identF), reads=[bf("cst")], writes=[bf("identB")])
    op(DVE, lambda e: e.tensor_tensor(out=DI[:], in0=identF.unsqueeze(1).to_broadcast([128, 8, 128]),
                                      in1=C("D_bc").unsqueeze(2).to_broadcast([128, 8, 128]), op=ALU.mult),
       reads=[bf("cst")], writes=[bf("DI")])
    op(ACT, lambda e: e.activation(out=a8[:], in_=C("alog8")[0:8, :], func=AF.Exp), reads=[bf("cst")], writes=[bf("a8")])
    op(DVE, lambda e: e.tensor_scalar(out=a8[:], in0=a8[:], scalar1=-1.0, scalar2=None, op0=ALU.mult), reads=[bf("a8")], writes=[bf("a8")])
    op(POOL, lambda e: e.memset(cmask[:], 1.0), writes=[bf("cmask")])
    op(POOL, lambda e: e.memset(cmask[:].rearrange("p (c j) -> p c j", c=4)[:, :, 0:1], 0.0), writes=[bf("cmask")])
    op(DVE, lambda e: e.tensor_copy(out=GWb[:], in_=C2("GW")), reads=[S5], writes=[bf("GWb")])

    lvl[0] = 1
    def s5tile(name, shape=(128, 16)):
        return ar("pro", "s5_" + name, list(shape), bufname="s5c")
    dtb = s5tile("dtb"); a_ = s5tile("a"); th = s5tile("th"); rr = s5tile("rr")
    cc = s5tile("cc"); ss_ = s5tile("ss"); t1 = s5tile("t1"); t2 = s5tile("t2"); t3 = s5tile("t3")
    lbr = s5tile("lbr"); lbi = s5tile("lbi"); den = s5tile("den"); qre = s5tile("qre"); qim = s5tile("qim")
    Bbre = s5tile("Bbre", (128, 16, 16)); Bbim = s5tile("Bbim", (128, 16, 16))
    u1 = s5tile("u1", (128, 16, 16)); u2 = s5tile("u2", (128, 16, 16))
    Pr = s5tile("Pr", (128, 9, 16)); Pi = s5tile("Pi", (128, 9, 16))
    Qr = s5tile("Qr", (128, 8, 16)); Qi = s5tile("Qi", (128, 8, 16))
    def s5tile_b(name, shape):
        return ar("pro", "s5_" + name, list(shape), BF16, bufname="s5c")
    Zre = s5tile_b("Zre", (128, 16, 8, 16)); Zim = s5tile_b("Zim", (128, 16, 8, 16))
    Ykre = s5tile_b("Ykre", (128, 16, 8, 16)); Ykim = s5tile_b("Ykim", (128, 16, 8, 16))
    Fm = ar("pro", "Fm", [128, 2, 16, 128], BF16, bufname="s5c", at="s5_Ykre")
    halfpi = s5tile("halfpi", (128, 1))

    def dv(fn, r=(S5, ), w=(S5, )):
        op(DVE, fn, reads=list(r) + [bf("cst")], writes=list(w))

    def ac(fn, r=(S5, ), w=(S5, )):
        op(ACT, fn, reads=list(r) + [bf("cst")], writes=list(w))
    TT = lambda o, a, b, o_: (lambda e: e.tensor_tensor(out=o, in0=a, in1=b, op=o_))
    op(POOL, lambda e: e.memset(halfpi[:], float(np.pi / 2)), writes=[S5])
    ac(lambda e: e.activation(out=dtb[:], in_=C2("logdt"), func=AF.Exp))
    dv(TT(a_[:], C2("lamre"), dtb[:], ALU.mult))
    dv(TT(th[:], C2("lamim"), dtb[:], ALU.mult))
    ac(lambda e: e.activation(out=rr[:], in_=a_[:], func=AF.Exp))
    import math
    dv(lambda e: e.tensor_scalar(out=t1[:], in0=th[:], scalar1=1.0 / 16, scalar2=None, op0=ALU.mult))
    dv(TT(t2[:], t1[:], t1[:], ALU.mult))
    sc = [(-1.0) ** k / math.factorial(2 * k + 1) for k in range(7)]
    dv(lambda e: e.tensor_scalar(out=t3[:], in0=t2[:], scalar1=sc[6], scalar2=None, op0=ALU.mult))
    for k in (5, 4, 3, 2, 1):
        dv(lambda e, k=k: e.scalar_tensor_tensor(out=t3[:], in0=t3[:], scalar=sc[k], in1=t2[:], op0=ALU.add, op1=ALU.mult))
    dv(lambda e: e.scalar_tensor_tensor(out=ss_[:], in0=t3[:], scalar=1.0, in1=t1[:], op0=ALU.add, op1=ALU.mult))
    cs_ = [(-1.0) ** k / math.factorial(2 * k) for k in range(8)]
    dv(lambda e: e.tensor_scalar(out=t3[:], in0=t2[:], scalar1=cs_[7], scalar2=None, op0=ALU.mult))
    for k in (6, 5, 4, 3, 2, 1):
        dv(lambda e, k=k: e.scalar_tensor_tensor(out=t3[:], in0=t3[:], scalar=cs_[k], in1=t2[:], op0=ALU.add, op1=ALU.mult))
    dv(lambda e: e.tensor_scalar(out=cc[:], in0=t3[:], scalar1=1.0, scalar2=None, op0=ALU.add))
    for _ in range(4):
        dv(TT(t1[:], cc[:], cc[:], ALU.mult))
        dv(TT(t2[:], ss_[:], ss_[:], ALU.mult))
        dv(TT(t3[:], cc[:], ss_[:], ALU.mult))
        dv(TT(cc[:], t1[:], t2[:], ALU.subtract))
        dv(lambda e: e.tensor_scalar(out=ss_[:], in0=t3[:], scalar1=2.0, scalar2=None, op0=ALU.mult))
    dv(TT(lbr[:], rr[:], cc[:], ALU.mult))
    dv(TT(lbi[:], rr[:], ss_[:], ALU.mult))
    dv(TT(t1[:], C2("lamre"), C2("lamre"), ALU.mult))
    dv(TT(t2[:], C2("lamim"), C2("lamim"), ALU.mult))
    dv(TT(den[:], t1[:], t2[:], ALU.add))
    dv(lambda e: e.reciprocal(out=den[:], in_=den[:]))
    dv(lambda e: e.tensor_scalar(out=t3[:], in0=lbr[:], scalar1=-1.0, scalar2=None, op0=ALU.add))
    dv(TT(t1[:], t3[:], C2("lamre"), ALU.mult))
    dv(TT(t2[:], lbi[:], C2("lamim"), ALU.mult))
    dv(TT(t1[:], t1[:], t2[:], ALU.add))
    dv(TT(qre[:], t1[:], den[:], ALU.mult))
    dv(TT(t1[:], lbi[:], C2("lamre"), ALU.mult))
    dv(TT(t2[:], t3[:], C2("lamim"), ALU.mult))
    dv(TT(t1[:], t1[:], t2[:], ALU.subtract))
    dv(TT(qim[:], t1[:], den[:], ALU.mult))
    bc16 = lambda ap: ap.unsqueeze(2).to_broadcast([128, 16, 16])
    dv(TT(u1[:], C2("Bre"), bc16(qre[:]), ALU.mult))
    dv(TT(u2[:], C2("Bim"), bc16(qim[:]), ALU.mult))
    dv(TT(Bbre[:], u1[:], u2[:], ALU.subtract))
    dv(TT(u1[:], C2("Bim"), bc16(qre[:]), ALU.mult))
    dv(TT(u2[:], C2("Bre"), bc16(qim[:]), ALU.mult))
    dv(TT(Bbim[:], u1[:], u2[:], ALU.add))
    op(POOL, lambda e: e.memset(Pr[:, 0, :], 1.0), writes=[S5])
    op(POOL, lambda e: e.memset(Pi[:, 0, :], 0.0), writes=[S5])
    op(POOL, lambda e: e.memset(Qr[:, 0, :], 1.0), writes=[S5])
    op(POOL, lambda e: e.memset(Qi[:, 0, :], 0.0), writes=[S5])

    T1, T2, T3, U1, U2, OUT = bf("s5_T1"), bf("s5_T2"), bf("s5_T3"), bf("s5_U1"), bf("s5_U2"), bf("s5_OUT")
    PB = lambda nm, k: bf("s5_%s%d" % (nm, k))
    fine = [T1, T2, T3, U1, U2, OUT]

    def cmul(outr, outi, ar, ai, br, bi, RI, II, RO, IO):
        dv(TT(t1[:], ar, br, ALU.mult), r=(S5, RI), w=(T1, ))
        dv(TT(t2[:], ai, bi, ALU.mult), r=(S5, II), w=(T2, ))
        dv(TT(t3[:], ar, bi, ALU.mult), r=(S5, RI), w=(T3, ))
        dv(TT(outr, t1[:], t2[:], ALU.subtract), r=(T1, T2), w=(RO, ))
        dv(TT(t1[:], ai, br, ALU.mult), r=(S5, II), w=(T1, ))
        dv(TT(outi, t3[:], t1[:], ALU.add), r=(T3, T1), w=(IO, ))
        fine.extend([RO, IO])
    for k in range(8):
        cmul(Pr[:, k + 1, :], Pi[:, k + 1, :], Pr[:, k, :], Pi[:, k, :], lbr[:], lbi[:], PB("Pr", k), PB("Pi", k), PB("Pr", k + 1), PB("Pi", k + 1))
    ivr = s5tile("ivr"); ivi = s5tile("ivi")
    dv(TT(t1[:], lbr[:], lbr[:], ALU.mult))
    dv(TT(t2[:], lbi[:], lbi[:], ALU.mult))
    dv(TT(t1[:], t1[:], t2[:], ALU.add))
    dv(lambda e: e.reciprocal(out=t1[:], in_=t1[:]))
    dv(TT(ivr[:], lbr[:], t1[:], ALU.mult))
    dv(TT(ivi[:], lbi[:], t1[:], ALU.mult))
    dv(lambda e: e.tensor_scalar(out=ivi[:], in0=ivi[:], scalar1=-1.0, scalar2=None, op0=ALU.mult))
    for k in range(7):
        cmul(Qr[:, k + 1, :], Qi[:, k + 1, :], Qr[:, k, :], Qi[:, k, :], ivr[:], ivi[:], PB("Qr", k), PB("Qi", k), PB("Qr", k + 1), PB("Qi", k + 1))

    def cmul_bc(outr, outi, pr, pi, xr, xi, PRB, PIB, neg_im=False):
        dv(TT(u1[:], xr, bc16(pr), ALU.mult), r=(S5, PRB), w=(U1, ))
        dv(TT(u2[:], xi, bc16(pi), ALU.mult), r=(S5, PIB), w=(U2, ))
        dv(TT(outr, u1[:], u2[:], ALU.subtract), r=(U1, U2), w=(OUT, ))
        dv(TT(u1[:], xi, bc16(pr), ALU.mult), r=(S5, PRB), w=(U1, ))
        dv(TT(u2[:], xr, bc16(pi), ALU.mult), r=(S5, PIB), w=(U2, ))
        if neg_im:
            dv(lambda e: e.scalar_tensor_tensor(out=outi, in0=u1[:], scalar=-1.0, in1=u2[:], op0=ALU.mult, op1=ALU.subtract), r=(U1, U2), w=(OUT, ))
        else:
            dv(TT(outi, u1[:], u2[:], ALU.add), r=(U1, U2), w=(OUT, ))
    for s8 in range(8):
        cmul_bc(Zre[:, :, s8, :], Zim[:, :, s8, :], Pr[:, 7 - s8, :], Pi[:, 7 - s8, :], Bbre[:], Bbim[:], PB("Pr", 7 - s8), PB("Pi", 7 - s8))
    lvl[0] = 2
    Fm4 = [Fm[:, ri, :, :].rearrange("p g (t c) -> p g t c", t=8) for ri in range(2)]
    for t8 in range(8):
        cmul_bc(Fm4[0][:, :, t8, :], Fm4[1][:, :, t8, :], Pr[:, t8 + 1, :], Pi[:, t8 + 1, :], C2("Cre"), C2("Cim"), PB("Pr", t8 + 1), PB("Pi", t8 + 1), neg_im=True)
    for ri in range(2):
        for gl in range(2):
            dv(lambda e, ri=ri, gl=gl: e.tensor_scalar(out=Fg[:, ri, gl:32:2, :], in0=Fm[:, ri, :, :], scalar1=C2("rm")[:, gl:gl + 1], scalar2=None, op0=ALU.mult), r=(S5, OUT), w=(S5, bf("Fm")))
    for t8 in range(8):
        cmul_bc(Ykre[:, :, t8, :], Ykim[:, :, t8, :], Qr[:, 7 - t8, :], Qi[:, 7 - t8, :], C2("Cre"), C2("Cim"), PB("Qr", 7 - t8), PB("Qi", 7 - t8), neg_im=True)
    dv(lambda e: e.tensor_copy(out=t1[:, 0:1], in_=t1[:, 0:1]), r=tuple([S5] + fine), w=(S5, T1))
    lvl[0] = 3
    Zre2 = Zre[:].rearrange("p g s c -> p g (s c)")
    Zim2 = Zim[:].rearrange("p g s c -> p g (s c)")
    Ykre2 = Ykre[:].rearrange("p g s c -> p g (s c)")
    Ykim2 = Ykim[:].rearrange("p g s c -> p g (s c)")
    op(POOL, lambda e: e.memset(ZT[:].rearrange("p a g q -> p (a g q)"), 0.0), writes=[bf("ZT")])
    for ri, Z2 in enumerate((Zre2, Zim2)):
        def ev_zt(s0, cnt, view, tbb, ri=ri):
            op(ACT, lambda e: e.activation(out=ZT[:, ri, 2 * s0:2 * s0 + 2 * cnt:2, 0:64], in_=view[:, :, 0:64], func=AF.Copy), reads=[tbb], writes=[bf("ZT")])
            op(ACT, lambda e: e.activation(out=ZT[:, ri, 2 * s0 + 1:2 * s0 + 2 * cnt:2, 64:128], in_=view[:, :, 64:128], func=AF.Copy), reads=[tbb], writes=[bf("ZT")])
        pe_transposes(16, lambda gp, Z2=Z2: Z2[:, gp, :], 128, 128, [S5], ev_zt)
    lvl[0] = 4
    Zm = [[s5tile_b("Zm%d%d" % (ri, gl), (128, 16, 128)) for gl in range(2)] for ri in range(2)]
    for ri, Z2 in enumerate((Zre2, Zim2)):
        for gl in range(2):
            dv(lambda e, ri=ri, gl=gl, Z2=Z2: e.tensor_scalar(out=Zm[ri][gl][:], in0=Z2, scalar1=C2("rm")[:, gl:gl + 1], scalar2=None, op0=ALU.mult))
    for g0 in range(0, 32, 4):
        bk, bb_ = nbank()
        for gi in range(4):
            gg = g0 + gi
            gp, gl = gg // 2, gg % 2
            op(PE, lambda e, bk=bk, gi=gi, gp=gp, gl=gl: e.matmul(bk[:, gi * 128:(gi + 1) * 128], lhsT=Zm[0][gl][:, gp, :], rhs=Ykre2[:, gp, :], start=True, stop=False),
               reads=[S5], writes=[bb_], sig=False)
            op(PE, lambda e, bk=bk, gi=gi, gp=gp, gl=gl: e.matmul(bk[:, gi * 128:(gi + 1) * 128], lhsT=Zm[1][gl][:, gp, :], rhs=Ykim2[:, gp, :], start=False, stop=True),
               reads=[S5], writes=[bb_], sig=(gi == 3))
        op(DVE, lambda e, bk=bk, g0=g0: e.tensor_tensor(out=Km[:, g0:g0 + 4, :], in0=bk[:].rearrange("p (g j) -> p g j", g=4),
                                                         in1=C2("kmask").unsqueeze(1).to_broadcast([128, 4, 128]), op=ALU.mult),
           reads=[bb_, bf("cst")], writes=[bf("Km")])
        for gi in range(4):
            op(DVE, lambda e, gg=g0 + gi: e.scalar_tensor_tensor(out=Km[:, gg, :], in0=identF, scalar=C("d8")[:, gg:gg + 1], in1=Km[:, gg, :], op0=ALU.mult, op1=ALU.add),
               reads=[bf("Km"), bf("cst")], writes=[bf("Km")])
    lvl[0] = 5
    c8 = s5tile("c8"); s8t = s5tile("s8t")
    ac(lambda e: e.activation(out=r8[:], in_=a_[:], func=AF.Exp, scale=8.0))
    dv(lambda e: e.reciprocal(out=t1[:], in_=r8[:]))
    dv(TT(c8[:], Pr[:, 8, :], t1[:], ALU.mult))
    dv(TT(s8t[:], Pi[:, 8, :], t1[:], ALU.mult))
    dv(lambda e: e.tensor_copy(out=cosT[:, :, 0], in_=c8[:]))
    dv(lambda e: e.tensor_copy(out=sinT[:, :, 0], in_=s8t[:]))
    w1 = ar("pro", "s5_w1", [128, 16, JMAX // 2], F32, bufname="s5c", at="s5_Zre"); w2 = ar("pro", "s5_w2", [128, 16, JMAX // 2], F32, bufname="s5c", at="s5_Zim")
    n = 1
    while n < JMAX:
        bcn = lambda ap, n=n: ap.unsqueeze(2).to_broadcast([128, 16, n])
        cr = cosT[:, :, n - 1]; sr = sinT[:, :, n - 1]
        dv(TT(w1[:, :, :n], cosT[:, :, 0:n], bcn(cr), ALU.mult))
        dv(TT(w2[:, :, :n], sinT[:, :, 0:n], bcn(sr), ALU.mult))
        dv(TT(cosT[:, :, n:2 * n], w1[:, :, :n], w2[:, :, :n], ALU.subtract))
        dv(TT(w1[:, :, :n], cosT[:, :, 0:n], bcn(sr), ALU.mult))
        dv(TT(w2[:, :, :n], sinT[:, :, 0:n], bcn(cr), ALU.mult))
        dv(TT(sinT[:, :, n:2 * n], w1[:, :, :n], w2[:, :, :n], ALU.add))
        n *= 2
    dv(lambda e: e.tensor_copy(out=r8z[:], in_=r8[:].unsqueeze(2).to_broadcast([128, 16, JMAX])))
    op(POOL, lambda e: e.memset(r8z[:, :, 0:1], 0.0), reads=[S5], writes=[S5])
    dv(lambda e: e.tensor_copy(out=r8[:, 0:1], in_=r8[:, 0:1]), r=(S5, bf("ZT"), bf("Km")), w=(S5, bf("Fm"), bf("ZT"), bf("Km")))
    lvl[0] = 0
    S5C = [S5, bf("ZT"), bf("Fm"), bf("Km")]
    if DBG:
        def dump(name, ap, shape):
            d = dout("dbg_" + name, shape)
            op(POOL, lambda e: e.dma_start(out=d, in_=ap), reads=S5C, writes=[bf("dbgout")], dma="dbg_" + name)
        dump("lbr", lbr[:], [128, 16]); dump("lbi", lbi[:], [128, 16]); dump("qre", qre[:], [128, 16]); dump("qim", qim[:], [128, 16])
        dump("Pr", Pr[:], [128, 9, 16]); dump("Pi", Pi[:], [128, 9, 16]); dump("Qr", Qr[:], [128, 8, 16]); dump("Qi", Qi[:], [128, 8, 16])
        dump("cosT", cosT[:], [128, 16, JMAX]); dump("sinT", sinT[:], [128, 16, JMAX]); dump("r8", r8[:], [128, 16]); dump("r8z", r8z[:], [128, 16, JMAX])
        dump("Km", Km[:], [128, 32, 128]); dump("ZT", ZT[:], [128, 2, 32, 128]); dump("Fg", Fg[:], [128, 2, 32, 128])
        dump("Bbre", Bbre[:], [128, 16, 16]); dump("Zre", Zre[:], [128, 16, 8, 16]); dump("Ykre", Ykre[:], [128, 16, 8, 16])

    ring_i = [0]
    b_ring = [Buf("ring%d" % i) for i in range(RING)]

    def wload(src_ap, srcbuf):
        i = ring_i[0] % RING
        ring_i[0] += 1
        op(SP, lambda e, i=i, src_ap=src_ap: e.dma_start(out=ring[i][:], in_=src_ap), reads=[srcbuf], writes=[b_ring[i]], dma=("ring", i))
        return ring[i], b_ring[i]

    XR = [bf("xres%d" % b) for b in range(4)]

    def stage_rstd(blocks, npart):
        for b in blocks:
            op(DVE, lambda e, b=b: e.tensor_scalar(out=stt[:npart, b, 1:2], in0=stt[:npart, b, 0:1], scalar1=1.0 / 1024, scalar2=EPS, op0=ALU.mult, op1=ALU.add),
               reads=[bf("stt")], writes=[bf("stt")])
        for b in blocks:
            op(ACT, lambda e, b=b: e.activation(out=stt[:npart, b, 2:3], in_=stt[:npart, b, 1:2], func=AF.Ln), reads=[bf("stt")], writes=[bf("stt")])
        for b in blocks:
            op(ACT, lambda e, b=b: e.activation(out=stt[:npart, b, 3:4], in_=stt[:npart, b, 2:3], func=AF.Exp, scale=-0.5), reads=[bf("stt")], writes=[bf("stt")])

    def norm_transpose_all(blocks, npart, dstT, dstbuf, wk_name):
        for b in blocks:
            op(ACT, lambda e, b=b: e.activation(out=junk[:npart, :], in_=xres[:npart, b, :], func=AF.Square, accum_out=stt[:npart, b, 0:1]),
               reads=[XR[b]], writes=[bf("junk"), bf("stt")])
        stage_rstd(blocks, npart)
        for b in blocks:
            xb = xb16s[b % 2]
            XB_ = bf("xb16_%d" % (b % 2))
            tcols = slice(b * 128, b * 128 + npart)
            op(ACT, lambda e, b=b, xb=xb: e.activation(out=xb[:npart, :], in_=xres[:npart, b, :], func=AF.Copy, scale=stt[:npart, b, 3:4]),
               reads=[XR[b], bf("stt")], writes=[XB_])

            def ev(s0, cnt, view, tbb, tcols=tcols):
                op(DVE, lambda e: e.tensor_tensor(out=dstT[:, s0:s0 + cnt, tcols], in0=view, in1=C(wk_name)[:, s0:s0 + cnt].unsqueeze(2).to_broadcast([128, cnt, npart]), op=ALU.mult),
                   reads=[tbb, bf("cst")], writes=[dstbuf])
            pe_transposes(8, lambda k, xb=xb: xb[:npart, k * 128:(k + 1) * 128], npart, 128, [XB_], ev)

    def run_tile(xsrc, ydst, nt, first, seq):
        nb = max(1, nt // 128)
        ch = min(128, nt)
        J = nt // 8
        if nt >= 128:
            for b in range(nb):
                op(POOL, lambda e, b=b: e.dma_start(out=xres[:, b, :], in_=xsrc[b * 128:(b + 1) * 128, :]), writes=[XR[b]], dma=("ld_x", b))
        else:
            op(POOL, lambda e: e.dma_start(out=xres[:nt, 0, :], in_=xsrc), writes=XR, dma=("ld_x", 0))
        if STAGE < 1:
            return
        norm_transpose_all(list(range(nb)), ch, xnT, bf("xnT"), "wpre_k")
        wz = [wload(wsc_d[c], bf("wsc")) for c in (0, 1)]
        for b in range(nb):
            bk, bb_ = nbank()
            for hf in range(2):
                for k in range(8):
                    op(PE, lambda e, b=b, hf=hf, k=k, bk=bk: e.matmul(bk[:ch, hf * 256:(hf + 1) * 256], lhsT=xnT[:, k, b * 128:b * 128 + ch],
                                                                      rhs=wz[hf][0][:, k * 256:(k + 1) * 256], start=(k == 0), stop=(k == 7)),
                       reads=[bf("xnT"), wz[hf][1]], writes=[bb_], sig=(k == 7 and hf == 1))
            op(ACT, lambda e, b=b, bk=bk: e.activation(out=zs[:ch, b, :], in_=bk[:ch, :], func=AF.Silu), reads=[bb_], writes=[bf("zs")])
        for c in range(4):
            wr, wrb = wload(wsc_d[2 + c], bf("wsc"))
            for j in range(2):
                blk = 2 * c + j
                bk, bb_ = nbank()
                for k in range(8):
                    op(PE, lambda e, k=k, j=j, bk=bk, wr=wr: e.matmul(bk[:, :nt], lhsT=wr[:, k * 256 + j * 128:k * 256 + (j + 1) * 128], rhs=xnT[:, k, :nt],
                                                                      start=(k == 0), stop=(k == 7)),
                       reads=[bf("xnT"), wrb], writes=[bb_], sig=(k == 7))
                wi = blk % 2
                XB = bf("xbw%d" % wi)
                xw_ = xbw[wi]
                op(POOL, lambda e, blk=blk, xw_=xw_: e.tensor_copy(out=xw_[:, 0:3], in_=xh[:, blk, :]), reads=[bf("xh")], writes=[XB])
                op(ACT, lambda e, xw_=xw_, bk=bk: e.activation(out=xw_[:, 3:3 + nt], in_=bk[:, :nt], func=AF.Copy), reads=[bb_], writes=[XB])
                op(POOL, lambda e, blk=blk, xw_=xw_: e.tensor_copy(out=xh[:, blk, :], in_=xw_[:, nt:nt + 3]), reads=[XB], writes=[bf("xh")])
                cwv = C("cw")
                op(ACT, lambda e, blk=blk, xw_=xw_: e.activation(out=ct[0][:, :nt], in_=xw_[:, 0:nt], func=AF.Identity, scale=cwv[:, blk, 0:1], bias=C("cb")[:, blk:blk + 1]),
                   reads=[XB, bf("cst")], writes=[bf("ct0")])
                op(DVE, lambda e, blk=blk, xw_=xw_: e.scalar_tensor_tensor(out=ct[1][:, :nt], in0=xw_[:, 1:1 + nt], scalar=cwv[:, blk, 1:2], in1=ct[0][:, :nt], op0=ALU.mult, op1=ALU.add),
                   reads=[XB, bf("ct0")], writes=[bf("ct1")])
                op(DVE, lambda e, blk=blk, xw_=xw_: e.scalar_tensor_tensor(out=ct[0][:, :nt], in0=xw_[:, 2:2 + nt], scalar=cwv[:, blk, 2:3], in1=ct[1][:, :nt], op0=ALU.mult, op1=ALU.add),
                   reads=[XB, bf("ct1")], writes=[bf("ct0")])
                op(DVE, lambda e, blk=blk, bk=bk: e.scalar_tensor_tensor(out=ct[1][:, :nt], in0=bk[:, :nt], scalar=cwv[:, blk, 3:4], in1=ct[0][:, :nt], op0=ALU.mult, op1=ALU.add),
                   reads=[bb_, bf("ct0")], writes=[bf("ct1")])
                op(ACT, lambda e, blk=blk: e.activation(out=xbcT[:, blk, :nt], in_=ct[1][:, :nt], func=AF.Silu), reads=[bf("ct1")], writes=[bf("xbcT")])
        wr, wrb = wload(wsc_d[6], bf("wsc"))
        bk, bb_ = nbank()
        for k in range(8):
            op(PE, lambda e, k=k, bk=bk, wr=wr: e.matmul(bk[0:8, :nt], lhsT=wr[:, k * 256:k * 256 + 8], rhs=xnT[:, k, :nt], start=(k == 0), stop=(k == 7)),
               reads=[bf("xnT"), wrb], writes=[bb_], sig=(k == 7))
        op(ACT, lambda e, bk=bk: e.activation(out=dtmp[:, :nt], in_=bk[0:8, :nt], func=AF.Exp, bias=C("dtb8")[0:8, :]), reads=[bb_, bf("cst")], writes=[bf("dtmp")])
        op(ACT, lambda e: e.activation(out=dtT[:, :nt], in_=dtmp[:, :nt], func=AF.Ln, bias=1.0), reads=[bf("dtmp")], writes=[bf("dtT")])
        op(DVE, lambda e: e.tensor_scalar(out=dtaT[:, :nt], in0=dtT[:, :nt], scalar1=a8[:, 0:1], scalar2=None, op0=ALU.mult), reads=[bf("dtT"), bf("a8")], writes=[bf("dtaT")])
        op(DVE, lambda e: e.tensor_tensor_scan(out=csT[:, :nt], data0=cmask[:, :nt], data1=dtaT[:, :nt], initial=0.0, op0=ALU.mult, op1=ALU.add),
           reads=[bf("dtaT"), bf("cmask")], writes=[bf("csT")])
        bk, bb_ = nbank()
        for c in range(nb):
            cols = slice(c * 128, c * 128 + ch)
            op(PE, lambda e, c=c, cols=cols, bk=bk: e.transpose(out=bk[:ch, c * 16:c * 16 + 8], in_=dtT[0:8, cols], identity=identF[0:8, 0:8]),
               reads=[bf("dtT"), bf("cst")], writes=[bb_], sig=False)
            op(PE, lambda e, c=c, cols=cols, bk=bk: e.transpose(out=bk[:ch, c * 16 + 8:c * 16 + 16], in_=csT[0:8, cols], identity=identF[0:8, 0:8]),
               reads=[bf("csT"), bf("cst")], writes=[bb_], sig=(c == nb - 1))
        op(DVE, lambda e, bk=bk: e.tensor_copy(out=tk[:ch, 0:nb, :], in_=bk[:ch, 0:nb * 16].rearrange("p (c x) -> p c x", c=nb)), reads=[bb_], writes=[bf("tk")])
        op(ACT, lambda e: e.activation(out=ek[:ch, 0:nb, :], in_=tk[:ch, 0:nb, 8:16], func=AF.Exp), reads=[bf("tk")], writes=[bf("ek")])
        if STAGE < 2:
            return
        def ssd_front(c):
            cols = slice(c * 128, c * 128 + ch)
            par = c % 2
            xsb_tok, Mb, wk2, ck = xsb_toks[par], Mbs[par], wk2s[par], cks[par]
            XSB, MBB, WK2, CKB = bf("xsb_tok%d" % par), bf("Mb%d" % par), bf("wk2_%d" % par), bf("ck%d" % par)
            def ev_x(s0, cnt, view, tbb):
                op(DVE, lambda e: e.tensor_copy(out=xsb_tok[:ch, s0:s0 + cnt, :], in_=view), reads=[tbb], writes=[XSB])
            pe_transposes(6, lambda j, cols=cols: xbcT[:, j, cols], 128, ch, [bf("xbcT")], ev_x)
            bS, bSb = nbank()
            for g in range(2):
                op(PE, lambda e, g=g, cols=cols, bS=bS: e.matmul(bS[:ch, g * 128:g * 128 + ch], lhsT=xbcT[:, 4 + g, cols], rhs=xbcT[:, 6 + g, cols], start=True, stop=True),
                   reads=[bf("xbcT")], writes=[bSb], sig=(g == 1))
            bR = [nbank(), nbank()]
            for h in range(8):
                op(PE, lambda e, bR=bR, h=h, cols=cols: e.matmul(bR[h // 4][0][:, (h % 4) * 128:(h % 4) * 128 + ch], lhsT=C("selh")[0:8, h, :], rhs=csT[0:8, cols], start=True, stop=True),
                   reads=[bf("csT"), bf("cst")], writes=[bR[h // 4][1]], sig=(h % 4 == 3))
            for h in range(8):
                op(DVE, lambda e, bR=bR, h=h, c=c: e.scalar_tensor_tensor(out=arg[:ch, h, :ch], in0=bR[h // 4][0][:ch, (h % 4) * 128:(h % 4) * 128 + ch], scalar=tk[:ch, c, 8 + h:9 + h],
                                                                   in1=C("negmask")[:ch, :ch], op0=ALU.subtract, op1=ALU.add),
                   reads=[bR[h // 4][1], bf("tk"), bf("cst")], writes=[bf("arg")])
            for hh in range(2):
                op(DVE, lambda e, bR=bR, hh=hh, c=c: e.tensor_tensor(out=wk[:ch, hh * 4:hh * 4 + 4], in0=bR[hh][0][:ch, :].rearrange("p (h l) -> p h l", h=4)[:, :, ch - 1],
                                                              in1=tk[:ch, c, 8 + hh * 4:12 + hh * 4], op=ALU.subtract),
                   reads=[bR[hh][1], bf("tk")], writes=[bf("wk")])
                op(ACT, lambda e, bR=bR, hh=hh: e.activation(out=ck[:, hh * 4:hh * 4 + 4], in_=bR[hh][0][:, :].rearrange("p (h l) -> p h l", h=4)[:, :, ch - 1], func=AF.Exp),
                   reads=[bR[hh][1]], writes=[CKB])
            op(ACT, lambda e: e.activation(out=wk2[:ch, :], in_=wk[:ch, :], func=AF.Exp), reads=[bf("wk")], writes=[WK2])
            op(DVE, lambda e, c=c: e.tensor_tensor(out=wk2[:ch, :], in0=wk2[:ch, :], in1=tk[:ch, c, 0:8], op=ALU.mult), reads=[WK2, bf("tk")], writes=[WK2])
            op(ACT, lambda e: e.activation(out=Eb[:ch, :, :ch], in_=arg[:ch, :, :ch], func=AF.Exp), reads=[bf("arg")], writes=[bf("Eb")])
            for h in range(8):
                g = h // 4
                op(DVE, lambda e, h=h, g=g, c=c, bS=bS: e.scalar_tensor_tensor(out=Mb[:ch, h, :ch], in0=Eb[:ch, h, :ch], scalar=tk[:ch, c, h:h + 1],
                                                                               in1=bS[:ch, g * 128:g * 128 + ch], op0=ALU.mult, op1=ALU.mult),
                   reads=[bf("Eb"), bf("tk"), bSb], writes=[MBB])

        def ssd_back(c):
            cols = slice(c * 128, c * 128 + ch)
            par = c % 2
            xsb_tok, Mb, wk2, ck = xsb_toks[par], Mbs[par], wk2s[par], cks[par]
            XSB, MBB, WK2, CKB = bf("xsb_tok%d" % par), bf("Mb%d" % par), bf("wk2_%d" % par), bf("ck%d" % par)
            bY, bYb = nbank()
            for h in range(8):
                op(PE, lambda e, h=h, bY=bY: e.matmul(bY[:ch, h * 64:(h + 1) * 64], lhsT=DI[:ch, h, :ch], rhs=xsb_tok[:ch, h // 2, (h % 2) * 64:(h % 2) * 64 + 64], start=True, stop=False),
                   reads=[bf("DI"), XSB], writes=[bYb], sig=False)
                op(PE, lambda e, h=h, bY=bY: e.matmul(bY[:ch, h * 64:(h + 1) * 64], lhsT=Mb[:ch, h, :ch], rhs=xsb_tok[:ch, h // 2, (h % 2) * 64:(h % 2) * 64 + 64], start=False, stop=True),
                   reads=[MBB, XSB], writes=[bYb], sig=(h == 7))
            bO, bOb = nbank()
            for g in range(2):
                op(PE, lambda e, g=g, cols=cols, bO=bO: e.matmul(bO[:ch, g * 256:(g + 1) * 256], lhsT=xbcT[:, 6 + g, cols], rhs=stB[:, g * 256:(g + 1) * 256], start=True, stop=True),
                   reads=[bf("xbcT"), bf("stB")], writes=[bOb], sig=(g == 1))
            op(DVE, lambda e, c=c, bO=bO: e.tensor_tensor(out=ytmp[:ch, :].rearrange("p (h x) -> p h x", h=8), in0=bO[:ch, :].rearrange("p (h x) -> p h x", h=8),
                                                          in1=ek[:ch, c, :].unsqueeze(2).to_broadcast([ch, 8, 64]), op=ALU.mult),
               reads=[bOb, bf("ek")], writes=[bf("ytmp")])
            op(DVE, lambda e, bY=bY: e.tensor_tensor(out=ytmp2[:ch, :], in0=ytmp[:ch, :], in1=bY[:ch, :], op=ALU.add), reads=[bf("ytmp"), bYb], writes=[bf("ytmp2")])
            op(DVE, lambda e, c=c: e.tensor_tensor(out=gbuf[:ch, :], in0=ytmp2[:ch, :], in1=zs[:ch, c, :], op=ALU.mult), reads=[bf("ytmp2"), bf("zs")], writes=[bf("gbuf")])
            for g in range(2):
                op(ACT, lambda e, g=g: e.activation(out=junk[:ch, g * 256:(g + 1) * 256], in_=gbuf[:ch, g * 256:(g + 1) * 256], func=AF.Square, accum_out=gs[:ch, g:g + 1]),
                   reads=[bf("gbuf")], writes=[bf("junk"), bf("gs")])
            op(DVE, lambda e: e.tensor_scalar(out=gs[:ch, 2:4], in0=gs[:ch, 0:2], scalar1=1.0 / 256, scalar2=EPS, op0=ALU.mult, op1=ALU.add), reads=[bf("gs")], writes=[bf("gs")])
            op(ACT, lambda e: e.activation(out=gs[:ch, 4:6], in_=gs[:ch, 2:4], func=AF.Ln), reads=[bf("gs")], writes=[bf("gs")])
            op(ACT, lambda e: e.activation(out=gs[:ch, 6:8], in_=gs[:ch, 4:6], func=AF.Exp, scale=-0.5), reads=[bf("gs")], writes=[bf("gs")])
            for g in range(2):
                op(ACT, lambda e, g=g: e.activation(out=gn[:ch, g * 256:(g + 1) * 256], in_=gbuf[:ch, g * 256:(g + 1) * 256], func=AF.Copy, scale=gs[:ch, 6 + g:7 + g]),
                   reads=[bf("gbuf"), bf("gs")], writes=[bf("gn")])
            def ev_g(s0, cnt, view, tbb, cols=cols):
                op(DVE, lambda e: e.tensor_tensor(out=mixinT[:, 0:4, cols], in0=view, in1=C("wssd_k").unsqueeze(2).to_broadcast([128, 4, ch]), op=ALU.mult),
                   reads=[tbb, bf("cst")], writes=[bf("mixinT")])
            pe_transposes(4, lambda j: gn[:ch, j * 128:(j + 1) * 128], ch, 128, [bf("gn")], ev_g)
            op(DVE, lambda e: e.tensor_tensor(out=xw_tok[:ch, :].rearrange("p (h x) -> p h x", h=8), in0=xsb_tok[:ch, 0:4, :].rearrange("p a (b x) -> p (a b) x", b=2),
                                              in1=wk2[:ch, :].unsqueeze(2).to_broadcast([ch, 8, 64]), op=ALU.mult),
               reads=[XSB, WK2], writes=[bf("xw_tok")])
            bX, bXb = nbank()
            for g in range(2):
                op(PE, lambda e, g=g, bX=bX: e.matmul(bX[:, g * 256:(g + 1) * 256], lhsT=xsb_tok[:ch, 4 + g, :], rhs=xw_tok[:ch, g * 256:(g + 1) * 256], start=True, stop=True),
                   reads=[XSB, bf("xw_tok")], writes=[bXb], sig=(g == 1))
            op(DVE, lambda e: e.tensor_tensor(out=stT[:, :].rearrange("p (h x) -> p h x", h=8), in0=stT[:, :].rearrange("p (h x) -> p h x", h=8),
                                              in1=ck[:, :].unsqueeze(2).to_broadcast([128, 8, 64]), op=ALU.mult),
               reads=[bf("stT"), CKB], writes=[bf("stT")])
            op(DVE, lambda e, bX=bX: e.tensor_tensor(out=stT[:, :], in0=stT[:, :], in1=bX[:, :], op=ALU.add), reads=[bf("stT"), bXb], writes=[bf("stT")])
            op(ACT, lambda e: e.activation(out=stB[:, :], in_=stT[:, :], func=AF.Copy), reads=[bf("stT")], writes=[bf("stB")])


        ssd_front(0)
        for c in range(nb):
            if c + 1 < nb:
                ssd_front(c + 1)
            ssd_back(c)

        if STAGE < 3:
            return
        wu = [wload(wsc_d[7 + c], bf("wsc")) for c in (0, 1)]
        for s8 in range(8):
            bk, bb_ = nbank()
            for hf in range(2):
                for k in range(8):
                    op(PE, lambda e, s8=s8, hf=hf, k=k, bk=bk: e.matmul(bk[:J, hf * 256:(hf + 1) * 256], lhsT=xnT[:, k, s8:nt:8],
                                                                        rhs=wu[hf][0][:, k * 256:(k + 1) * 256], start=(k == 0), stop=(k == 7)),
                       reads=[bf("xnT"), wu[hf][1]], writes=[bb_], sig=(k == 7 and hf == 1))
            op(ACT, lambda e, s8=s8, bk=bk: e.activation(out=u_tok2[:J, :, s8, :], in_=bk[:J, :].rearrange("p (g c) -> p g c", g=32), func=AF.Copy),
               reads=[bb_], writes=[bf("u_tok2")])

        def ev_u(s0, cnt, view, tbb):
            op(ACT, lambda e: e.activation(out=U8[:, s0:s0 + cnt, :J], in_=view, func=AF.Copy), reads=[tbb], writes=[bf("U8")])
        pe_transposes(32, lambda g: u_tok2[:J, g, :, :].rearrange("p s c -> p (s c)"), J, 128, [bf("u_tok2")], ev_u)
        for hf in range(2):
            bSr = [nbank(), nbank()]
            for ri in range(2):
                for gpl in range(8):
                    gp = hf * 8 + gpl
                    for gl in range(2):
                        gg = 2 * gp + gl
                        op(PE, lambda e, bSr=bSr, ri=ri, gpl=gpl, gl=gl, gg=gg: e.matmul(bSr[ri][0][:, gpl * J:(gpl + 1) * J], lhsT=ZT[:, ri, gg, :], rhs=U8[:, gg, :J], start=(gl == 0), stop=(gl == 1)),
                           reads=[bf("ZT"), bf("U8")], writes=[bSr[ri][1]], sig=(gpl == 7 and gl == 1))
            gps = slice(hf * 8, hf * 8 + 8)
            Sre = bSr[0][0][:, :8 * J].rearrange("p (g j) -> p g j", g=8)
            Sim = bSr[1][0][:, :8 * J].rearrange("p (g j) -> p g j", g=8)
            RB = bf("R_")
            rd = [bSr[0][1], bSr[1][1]] + S5C
            op(DVE, lambda e, Sre=Sre, gps=gps: e.tensor_tensor(out=t_a[:, :, :J], in0=Sre, in1=cosT[:, gps, :J], op=ALU.mult), reads=rd, writes=[bf("t_a")])
            op(DVE, lambda e, Sim=Sim, gps=gps: e.tensor_tensor(out=t_b[:, :, :J], in0=Sim, in1=sinT[:, gps, :J], op=ALU.mult), reads=rd, writes=[bf("t_b")])
            op(DVE, lambda e: e.tensor_tensor(out=R_[:, 0, :, :J], in0=t_a[:, :, :J], in1=t_b[:, :, :J], op=ALU.add), reads=[bf("t_a"), bf("t_b")], writes=[RB])
            op(DVE, lambda e, Sim=Sim, gps=gps: e.tensor_tensor(out=t_a[:, :, :J], in0=Sim, in1=cosT[:, gps, :J], op=ALU.mult), reads=rd, writes=[bf("t_a")])
            op(DVE, lambda e, Sre=Sre, gps=gps: e.tensor_tensor(out=t_b[:, :, :J], in0=Sre, in1=sinT[:, gps, :J], op=ALU.mult), reads=rd, writes=[bf("t_b")])
            op(DVE, lambda e: e.tensor_tensor(out=R_[:, 1, :, :J], in0=t_a[:, :, :J], in1=t_b[:, :, :J], op=ALU.subtract), reads=[bf("t_a"), bf("t_b")], writes=[RB])
            for ri in range(2):
                op(DVE, lambda e, ri=ri, gps=gps: e.tensor_tensor(out=t_a[:, :, 0], in0=Hc[:, ri, gps], in1=r8[:, gps], op=ALU.mult), reads=[bf("Hc")] + S5C, writes=[bf("t_a")])
                op(DVE, lambda e, ri=ri: e.tensor_tensor(out=R_[:, ri, :, 0], in0=R_[:, ri, :, 0], in1=t_a[:, :, 0], op=ALU.add), reads=[bf("t_a"), RB], writes=[RB])
            for ri in range(2):
                if J == JMAX:
                    op(DVE, lambda e, ri=ri, gps=gps: e.tensor_tensor_scan(out=G_[:, ri, :, :].rearrange("p g j -> p (g j)"), data0=r8z[:, gps, :].rearrange("p g j -> p (g j)"),
                                                                           data1=R_[:, ri, :, :].rearrange("p g j -> p (g j)"), initial=0.0, op0=ALU.mult, op1=ALU.add),
                       reads=[RB] + S5C, writes=[bf("G_")])
                else:
                    for gpl in range(8):
                        op(DVE, lambda e, ri=ri, gpl=gpl, hf=hf: e.tensor_tensor_scan(out=G_[:, ri, gpl, :J], data0=r8z[:, hf * 8 + gpl, :J], data1=R_[:, ri, gpl, :J], initial=0.0,
                                                                                      op0=ALU.mult, op1=ALU.add),
                           reads=[RB] + S5C, writes=[bf("G_")])
            if DBG and first and seq == "p" and hf == 0:
                op(DVE, lambda e, bSr=bSr, Sre=Sre: e.tensor_copy(out=sgt[:, :8 * J].rearrange("p (g j) -> p g j", g=8), in_=Sre), reads=[bSr[0][1]], writes=[bf("sgt")])
                d_ = dout("dbg_S0", [128, 8, JMAX])
                op(POOL, lambda e, d_=d_: e.dma_start(out=d_, in_=sgt[:, :8 * J].rearrange("p (g j) -> p g j", g=8)), reads=[bf("sgt")], writes=[bf("dbgout")], dma="dbg_S0")
            GB = bf("G_")
            HB = bf("Hn")
            op(DVE, lambda e, gps=gps: e.tensor_tensor(out=t_a[:, :, :J], in0=G_[:, 0, :, :J], in1=cosT[:, gps, :J], op=ALU.mult), reads=[GB] + S5C, writes=[bf("t_a")])
            op(DVE, lambda e, gps=gps: e.tensor_tensor(out=t_b[:, :, :J], in0=G_[:, 1, :, :J], in1=sinT[:, gps, :J], op=ALU.mult), reads=[GB] + S5C, writes=[bf("t_b")])
            op(DVE, lambda e, gps=gps: e.tensor_tensor(out=Hn[:, 0, gps, :J], in0=t_a[:, :, :J], in1=t_b[:, :, :J], op=ALU.subtract), reads=[bf("t_a"), bf("t_b")], writes=[HB])
            op(DVE, lambda e, gps=gps: e.tensor_tensor(out=t_a[:, :, :J], in0=G_[:, 1, :, :J], in1=cosT[:, gps, :J], op=ALU.mult), reads=[GB] + S5C, writes=[bf("t_a")])
            op(DVE, lambda e, gps=gps: e.tensor_tensor(out=t_b[:, :, :J], in0=G_[:, 0, :, :J], in1=sinT[:, gps, :J], op=ALU.mult), reads=[GB] + S5C, writes=[bf("t_b")])
            op(DVE, lambda e, gps=gps: e.tensor_tensor(out=Hn[:, 1, gps, :J], in0=t_a[:, :, :J], in1=t_b[:, :, :J], op=ALU.add), reads=[bf("t_a"), bf("t_b")], writes=[HB])
            if DBG and first and seq == "p" and hf == 0:
                for nm, ap, shp, rd in (("R0", R_[:], [128, 2, 8, JMAX], [bf("R_")]), ("G0", G_[:], [128, 2, 8, JMAX], [bf("G_")]), ("Hn0", Hn[:], [128, 2, 16, JMAX], [bf("Hn")])):
                    d_ = dout("dbg_" + nm, shp)
                    op(DVE, lambda e, d_=d_, ap=ap: e.dma_start(out=d_, in_=ap), reads=rd, writes=[bf("dbgout")], dma="dbg_" + nm)
        op(ACT, lambda e: e.activation(out=Hprev[:, :, :, 0], in_=Hc[:, :, :], func=AF.Copy), reads=[bf("Hc")], writes=[bf("Hprev")])
        op(ACT, lambda e: e.activation(out=Hprev[:, :, :, 1:J], in_=Hn[:, :, :, 0:J - 1], func=AF.Copy), reads=[bf("Hn")], writes=[bf("Hprev")])
        op(ACT, lambda e: e.activation(out=Hc[:, :, :], in_=Hn[:, :, :, J - 1], func=AF.Copy), reads=[bf("Hn"), bf("Hprev")], writes=[bf("Hc")])
        for g0 in range(0, 32, 4):
            bk, bb_ = nbank()
            for gi in range(4):
                gg = g0 + gi
                gp = gg // 2
                osl = slice(gi * 128, (gi + 1) * 128)
                op(PE, lambda e, gg=gg, bk=bk, osl=osl: e.matmul(bk[:J, osl], lhsT=U8[:, gg, :J], rhs=Km[:, gg, :], start=True, stop=False),
                   reads=[bf("Km"), bf("U8")], writes=[bb_], sig=False)
                op(PE, lambda e, gg=gg, gp=gp, bk=bk, osl=osl: e.matmul(bk[:J, osl], lhsT=Hprev[:, 0, gp, :J], rhs=Fg[:, 0, gg, :], start=False, stop=False),
                   reads=[bf("Fm"), bf("Hprev")], writes=[bb_], sig=False)
                op(PE, lambda e, gg=gg, gp=gp, bk=bk, osl=osl: e.matmul(bk[:J, osl], lhsT=Hprev[:, 1, gp, :J], rhs=Fg[:, 1, gg, :], start=False, stop=True),
                   reads=[bf("Fm"), bf("Hprev")], writes=[bb_], sig=(gi == 3))
            op(ACT, lambda e, g0=g0, bk=bk: e.activation(out=yj[:J, :, g0 * 16:(g0 + 4) * 16].rearrange("p t (g c) -> p g t c", g=4),
                                                         in_=bk[:J, :].rearrange("p (g t c) -> p g t c", g=4, t=8), func=AF.Copy),
               reads=[bb_], writes=[bf("yj")])
        for blk in range(4):
            def ev_t(s0, cnt, view, tbb, blk=blk):
                op(ACT, lambda e: e.activation(out=ygT[:, blk, :nt].rearrange("p (j t) -> p t j", t=8)[:, s0:s0 + cnt, :], in_=view, func=AF.Gelu_apprx_tanh),
                   reads=[tbb], writes=[bf("ygT")])
            pe_transposes(8, lambda t8, blk=blk: yj[:J, t8, blk * 128:(blk + 1) * 128], J, 128, [bf("yj")], ev_t)
        for blk in range(4):
            bA, bAb = nbank()
            bG, bGb = nbank()
            op(PE, lambda e, blk=blk, bA=bA: e.matmul(bA[:, :nt], lhsT=GWb[:, blk, 0, :], rhs=ygT[:, blk, :nt], start=True, stop=True), reads=[bf("GWb"), bf("ygT")], writes=[bAb])
            op(PE, lambda e, blk=blk, bG=bG: e.matmul(bG[:, :nt], lhsT=GWb[:, blk, 1, :], rhs=ygT[:, blk, :nt], start=True, stop=True), reads=[bf("GWb"), bf("ygT")], writes=[bGb])
            op(ACT, lambda e, blk=blk, bG=bG: e.activation(out=sgt[:, :nt], in_=bG[:, :nt], func=AF.Sigmoid, bias=C("gbb")[:, blk:blk + 1]), reads=[bGb, bf("cst")], writes=[bf("sgt")])
            op(DVE, lambda e, blk=blk, bA=bA: e.scalar_tensor_tensor(out=mixinT[:, 4 + blk, :nt], in0=bA[:, :nt], scalar=C("gba")[:, blk:blk + 1], in1=sgt[:, :nt], op0=ALU.add, op1=ALU.mult),
               reads=[bAb, bf("sgt"), bf("cst")], writes=[bf("mixinT")])

        if STAGE < 4:
            return
        def norm_res_all(bankmap, wbc_name):
            blocks = sorted(bankmap)
            for b in blocks:
                bk0, bb0, bk1, bb1 = bankmap[b]
                op(ACT, lambda e, b=b, bk0=bk0: e.activation(out=junk[:ch, 0:512], in_=bk0[:ch, :], func=AF.Square, accum_out=stt[:ch, b, 4:5]), reads=[bb0], writes=[bf("junk"), bf("stt")])
                op(ACT, lambda e, b=b, bk1=bk1: e.activation(out=junk[:ch, 512:1024], in_=bk1[:ch, :], func=AF.Square, accum_out=stt[:ch, b, 5:6]), reads=[bb1], writes=[bf("junk"), bf("stt")])
            for b in blocks:
                op(DVE, lambda e, b=b: e.tensor_tensor(out=stt[:ch, b, 0:1], in0=stt[:ch, b, 4:5], in1=stt[:ch, b, 5:6], op=ALU.add), reads=[bf("stt")], writes=[bf("stt")])
            stage_rstd(blocks, ch)
            wbc = C(wbc_name)
            for b in blocks:
                bk0, bb0, bk1, bb1 = bankmap[b]
                mt = mixtmps[b % 2]
                MT = bf("mixtmp%d" % (b % 2))
                op(DVE, lambda e, b=b, bk0=bk0, mt=mt: e.scalar_tensor_tensor(out=mt[:ch, 0:512], in0=bk0[:ch, :], scalar=stt[:ch, b, 3:4], in1=wbc[:ch, 0:512], op0=ALU.mult, op1=ALU.mult),
                   reads=[bb0, bf("stt"), bf("cst")], writes=[MT])
                op(DVE, lambda e, b=b, bk1=bk1, mt=mt: e.scalar_tensor_tensor(out=mt[:ch, 512:1024], in0=bk1[:ch, :], scalar=stt[:ch, b, 3:4], in1=wbc[:ch, 512:1024], op0=ALU.mult, op1=ALU.mult),
                   reads=[bb1, bf("stt"), bf("cst")], writes=[MT])
                op(POOL, lambda e, b=b, mt=mt: e.tensor_tensor(out=xres[:ch, b, :], in0=xres[:ch, b, :], in1=mt[:ch, :], op=ALU.add), reads=[XR[b], MT], writes=[XR[b]])

        allb = [(pb[i], b_pb[i]) for i in range(NB)] + [(pt[i], b_pt[i]) for i in range(2)]
        bankmap = {}
        for b in range(nb):
            bk0, bb0 = allb[2 * b]
            bk1, bb1 = allb[2 * b + 1]
            bankmap[b] = (bk0, bb0, bk1, bb1)
            for hf, (bk, bb_) in enumerate(((bk0, bb0), (bk1, bb1))):
                for k in range(8):
                    op(PE, lambda e, hf=hf, k=k, bk=bk, b=b: e.matmul(bk[:ch, :], lhsT=mixinT[:, k, b * 128:b * 128 + ch], rhs=wout_sb[:, k, hf * 512:(hf + 1) * 512], start=(k == 0), stop=(k == 7)),
                       reads=[bf("mixinT"), bf("wout_sb")], writes=[bb_], sig=(k == 7))
        norm_res_all(bankmap, "wpost_bc")
        norm_transpose_all(list(range(nb)), ch, xnT, bf("xnT"), "wffn_k")

        if STAGE < 5:
            return
        for m in range(22):
            wr, wrb = wload(wsc_d[NWIN + m], bf("wsc"))
            pre = []
            for j in range(2):
                blk = 2 * m + j
                bk, bb_ = nbank()
                for k in range(8):
                    op(PE, lambda e, k=k, j=j, bk=bk, wr=wr: e.matmul(bk[:, :nt], lhsT=wr[:, k * 256 + j * 128:k * 256 + (j + 1) * 128], rhs=xnT[:, k, :nt], start=(k == 0), stop=(k == 7)),
                       reads=[bf("xnT"), wrb], writes=[bb_], sig=(k == 7))
                ui = (2 * m + j) % 4
                UB = bf("upb%d" % ui)
                fwv = C("fw")
                op(POOL, lambda e, blk=blk, ui=ui: e.tensor_copy(out=upb[ui][:, 0:2], in_=fh[:, blk, :]), reads=[bf("fh")], writes=[UB])
                op(ACT, lambda e, ui=ui, bk=bk: e.activation(out=upb[ui][:, 2:2 + nt], in_=bk[:, :nt], func=AF.Copy), reads=[bb_], writes=[UB])
                op(POOL, lambda e, blk=blk, ui=ui: e.tensor_copy(out=fh[:, blk, :], in_=upb[ui][:, nt:nt + 2]), reads=[UB], writes=[bf("fh")])
                FA, FB = bf("fct%d" % ui), bf("fctb%d" % ui)
                op(ACT, lambda e, blk=blk, ui=ui: e.activation(out=fct[ui][:, :nt], in_=upb[ui][:, 0:nt], func=AF.Identity, scale=fwv[:, blk, 0:1], bias=C("fb")[:, blk:blk + 1]),
                   reads=[UB, bf("cst")], writes=[FA])
                op(DVE, lambda e, blk=blk, ui=ui: e.scalar_tensor_tensor(out=fct[ui][:, :nt], in0=upb[ui][:, 1:1 + nt], scalar=fwv[:, blk, 1:2], in1=fct[ui][:, :nt], op0=ALU.mult, op1=ALU.add),
                   reads=[UB, FA], writes=[FA])
                op(DVE, lambda e, blk=blk, ui=ui, bk=bk: e.scalar_tensor_tensor(out=fct[ui][:, :nt], in0=bk[:, :nt], scalar=fwv[:, blk, 2:3], in1=fct[ui][:, :nt], op0=ALU.mult, op1=ALU.add),
                   reads=[bb_, FA], writes=[FA])
                pre.append((ui, FA))
            gel = gels[m % 2]
            GB_ = bf("gel%d" % (m % 2))
            op(ACT, lambda e, gel=gel, ui=pre[0][0]: e.activation(out=gel[:, :nt], in_=fct[ui][:, :nt], func=AF.Gelu_apprx_tanh), reads=[pre[0][1]], writes=[GB_])
            op(POOL, lambda e, gel=gel, m=m, ui=pre[1][0]: e.tensor_tensor(out=actT[:, m, :nt], in0=gel[:, :nt], in1=fct[ui][:, :nt], op=ALU.mult), reads=[GB_, pre[1][1]], writes=[bf("actT")])
        if SUB < 2:
            return
        for b0 in range(0, nb, 2):
            bs = list(range(b0, min(nb, b0 + 2)))
            banks = {(b, hf): nbank() for b in bs for hf in range(2)}
            for pr in range(11):
                wr, wrb = wload(wdsc_d[pr], bf("wdsc"))
                for mm in range(2):
                    m = 2 * pr + mm
                    for b in bs:
                        for hf in range(2):
                            bk, bb_ = banks[(b, hf)]
                            op(PE, lambda e, m=m, mm=mm, b=b, hf=hf, bk=bk, wr=wr: e.matmul(bk[:ch, :], lhsT=actT[:, m, b * 128:b * 128 + ch], rhs=wr[:, mm * 1024 + hf * 512:mm * 1024 + (hf + 1) * 512],
                                                                                            start=(m == 0), stop=(m == 21)),
                               reads=[bf("actT"), wrb], writes=[bb_], sig=(mm == 1 and b == bs[-1] and hf == 1))
            norm_res_all({b: (banks[(b, 0)][0], banks[(b, 0)][1], banks[(b, 1)][0], banks[(b, 1)][1]) for b in bs}, "wpf_bc")
        if SUB < 3:
            return
        if nt >= 128:
            for b in range(nb):
                op(POOL, lambda e, b=b: e.dma_start(out=ydst[b * 128:(b + 1) * 128, :], in_=xres[:, b, :]), reads=[XR[b]], writes=[bf("yout%d" % b)], dma=("st_y", b))
        else:
            op(POOL, lambda e: e.dma_start(out=ydst, in_=xres[:nt, 0, :]), reads=XR, writes=[bf("yout0")], dma=("st_y", 0))

    def init_state(seq):
        if seq == "p":
            op(POOL, lambda e: e.memset(xh[:, :, :], 0.0), writes=[bf("xh")])
            op(POOL, lambda e: e.memset(stT[:, :], 0.0), writes=[bf("stT")])
            op(POOL, lambda e: e.memset(stB[:, :], 0.0), writes=[bf("stB")])
            op(POOL, lambda e: e.memset(Hc[:, :, :], 0.0), writes=[bf("Hc")])
            op(POOL, lambda e: e.memset(fh[:, :, :], 0.0), writes=[bf("fh")])
        else:
            op(SP, lambda e: e.dma_start(out=xh[:, :, :], in_=cconv_d), writes=[bf("xh")], dma="ld_st0")
            op(SP, lambda e: e.dma_start(out=Hc[:, 0, :], in_=cs5re_d), writes=[bf("Hc")], dma="ld_st1")
            op(SP, lambda e: e.dma_start(out=Hc[:, 1, :], in_=cs5im_d), writes=[bf("Hc")], dma="ld_st1")
            op(SP, lambda e: e.dma_start(out=fh[:, :, :], in_=cffn_d), writes=[bf("fh")], dma="ld_st3")
            op(SP, lambda e: e.dma_start(out=mixtmp[:, 0:512].rearrange("p (a n) -> p a n", a=4), in_=cssd_d), writes=[bf("mixtmp0")], dma="ld_st4")
            bk, bb_ = nbank()
            for j in range(4):
                op(PE, lambda e, j=j, bk=bk: e.transpose(out=bk[:, j * 128:(j + 1) * 128], in_=mixtmp[:, j * 128:(j + 1) * 128], identity=identF),
                   reads=[bf("mixtmp0"), bf("cst")], writes=[bb_], sig=(j == 3))
            op(DVE, lambda e, bk=bk: e.tensor_copy(out=stT[:, :], in_=bk[:, :]), reads=[bb_], writes=[bf("stT")])
            op(ACT, lambda e: e.activation(out=stB[:, :], in_=stT[:, :], func=AF.Copy), reads=[bf("stT")], writes=[bf("stB")])

    def store_state(sfx):
        op(SP, lambda e: e.dma_start(out=outs["conv" + sfx], in_=xh[:, :, :]), reads=[bf("xh")], writes=[bf("sout0" + sfx)], dma="st_s0" + sfx)
        op(SP, lambda e: e.dma_start(out=outs["s5re" + sfx], in_=Hc[:, 0, :]), reads=[bf("Hc")], writes=[bf("sout1" + sfx)], dma="st_s1" + sfx)
        op(SP, lambda e: e.dma_start(out=outs["s5im" + sfx], in_=Hc[:, 1, :]), reads=[bf("Hc")], writes=[bf("sout2" + sfx)], dma="st_s2" + sfx)
        op(SP, lambda e: e.dma_start(out=outs["ffn" + sfx], in_=fh[:, :, :]), reads=[bf("fh")], writes=[bf("sout3" + sfx)], dma="st_s3" + sfx)
        bk, bb_ = nbank()
        for j in range(4):
            op(PE, lambda e, j=j, bk=bk: e.transpose(out=bk[:, j * 128:(j + 1) * 128], in_=stT[:, j * 128:(j + 1) * 128], identity=identF),
               reads=[bf("stT"), bf("cst")], writes=[bb_], sig=(j == 3))
        op(DVE, lambda e, bk=bk: e.tensor_copy(out=mixtmp[:, 0:512], in_=bk[:, :]), reads=[bb_], writes=[bf("mixtmp0")])
        op(SP, lambda e: e.dma_start(out=outs["ssd" + sfx], in_=mixtmp[:, 0:512].rearrange("p (a n) -> p a n", a=4)), reads=[bf("mixtmp0")], writes=[bf("sout4" + sfx)], dma="st_s4" + sfx)

    init_state("p")
    for t0 in range(0, T, NT):
        run_tile(x_d[t0:t0 + NT, :], y_d[t0:t0 + NT, :], NT, t0 == 0, "p")
    store_state("p")
    init_state("s")
    run_tile(xs_d, ys_d, T_SAMPLE, True, "s")
    store_state("s")
    fin = [bf("yout%d" % b) for b in range(4)] + [bf("sout%d%s" % (i, sfx)) for i in range(5) for sfx in ("p", "s")]
    op(SP, lambda e: e.nop(), reads=fin, sig=False)
    op(POOL, lambda e: e.nop(), reads=fin, sig=False)
    P.emit()
    return nc


_CACHE = {}


def run(inputs, T=T_PROMPT):
    shared, pk, pk2 = host_shared(inputs)
    key = (T, pk.n)
    if key not in _CACHE:
        _CACHE[key] = build(T, pk, pk2)
    nc = _CACHE[key]
    in_maps = []
    for i in range(NCORES):
        d = host_core(inputs, i, T)
        d.update(shared)
        in_maps.append(d)
    res = run_bass_kernel_spmd(nc, in_maps, core_ids=list(range(NCORES)))
    R = res.results
    y = np.stack([R[i]["y"] for i in range(NCORES)])
    ys = np.stack([R[i]["ys"] for i in range(NCORES)])

    def conv(sfx):
        return np.stack([R[i]["o_conv" + sfx].transpose(2, 1, 0).reshape(3, 1024) for i in range(NCORES)])[None]

    def ssd(sfx):
        return np.stack([R[i]["o_ssd" + sfx].transpose(1, 0, 2).reshape(8, 64, 128) for i in range(NCORES)])[None]

    def s5(nm, sfx):
        return np.stack([unpg(R[i]["o_" + nm + sfx]) for i in range(NCORES)])[None]

    def ffn(sfx):
        out = []
        for i in range(NCORES):
            a = R[i]["o_ffn" + sfx].transpose(2, 1, 0).reshape(2, 5632)
            full = np.zeros((2, 5632), np.float32)
            full[:, FFN_COLS] = a
            out.append(full)
        return np.stack(out)[None]
    f = lambda a: np.ascontiguousarray(a, dtype=np.float32)
    return (f(y), f(ys),
            f(conv("p")), f(ssd("p")), f(s5("s5re", "p")), f(s5("s5im", "p")), f(ffn("p")),
            f(conv("s")), f(ssd("s")), f(s5("s5re", "s")), f(s5("s5im", "s")), f(ffn("s")))


def kernel(**inputs):
    return run(inputs, T_PROMPT)
```

```python
import numpy as np
import concourse.bass as bass
import concourse.mybir as mybir
from concourse.bass_utils import run_bass_kernel_spmd

F32 = mybir.dt.float32
BF16 = mybir.dt.bfloat16
AF = mybir.ActivationFunctionType
ALU = mybir.AluOpType

ENGS = ("pe", "act", "dve", "pool", "sp")
NO_SELF_SYNC = ("pe",)
NCORES = 8
T_PROMPT = 8192
T_SAMPLE = 32
NT = 512
EPS = 1e-6
NWIN = 9
NWUP = 22
RING = 5
STAGE = 9
SUB = 9
DBG = 0


class Buf:
    __slots__ = ("name", "w", "r", "al")

    def __init__(self, name):
        self.name = name
        self.w = None
        self.r = {}
        self.al = []


class Prog:
    def __init__(self, nc):
        self.nc = nc
        self.ops = {e: [] for e in ENGS}
        self.cnt = {}
        self.seen = {e: {} for e in ENGS}
        self.sems = {}

    def sem(self, key):
        if key not in self.sems:
            self.sems[key] = self.nc.alloc_semaphore("s_" + str(key).replace(" ", ""))
            self.cnt[key] = 0
        return self.sems[key]

    def op(self, eng, fn, reads=(), writes=(), sig=True, dma=None):
        waits = {}

        def need(ev):
            if ev is None:
                return
            k, v = ev
            if k == eng and eng in NO_SELF_SYNC:
                return
            if v > waits.get(k, 0):
                waits[k] = v
        for b in reads:
            need(b.w)
        for b in writes:
            for bb in [b] + b.al:
                if not (bb.w is not None and bb.w[0] == eng and eng in ("act", "dve")):
                    need(bb.w)
                for k, v in bb.r.items():
                    if k == eng and eng in ("act", "dve"):
                        continue
                    need((k, v))
        wl = []
        for k, v in waits.items():
            if self.seen[eng].get(k, 0) >= v:
                continue
            self.seen[eng][k] = v
            wl.append((k, v))
        if dma is not None:
            key = dma
            self.sem(key)
            self.cnt[key] += 16
            val = self.cnt[key]
            mode = ("dma", key)
        else:
            key = eng
            self.sem(key)
            if sig:
                self.cnt[key] += 1
                val = self.cnt[key]
                mode = ("sig", key)
            else:
                val = self.cnt[key] + 1
                mode = None
        self.ops[eng].append((wl, fn, mode))
        for b in reads:
            if b.r.get(key, 0) < val:
                b.r[key] = val
        for b in writes:
            b.w = (key, val)
            b.r = {}

    def emit(self):
        nc = self.nc
        with nc.Block() as block:
            def mk(ename):
                def body(e):
                    for wl, fn, mode in self.ops[ename]:
                        for k, v in wl:
                            e.wait_ge(self.sems[k], v)
                        ins = fn(e)
                        if mode is not None:
                            kind, key = mode
                            ins.then_inc(self.sems[key], 16 if kind == "dma" else 1)
                return body
            block.tensor(mk("pe"))
            block.scalar(mk("act"))
            block.vector(mk("dve"))
            block.gpsimd(mk("pool"))
            block.sync(mk("sp"))


class Packer:
    def __init__(self):
        self.items = []
        self.off = {}
        self.n = 0

    def add(self, name, arr):
        arr = np.asarray(arr, np.float32)
        if arr.shape[0] < 128:
            pad = np.zeros((128,) + arr.shape[1:], np.float32)
            pad[:arr.shape[0]] = arr
            arr = pad
        a2 = arr.reshape(128, -1)
        self.off[name] = (self.n, a2.shape[1], arr.shape[1:])
        self.items.append(a2)
        self.n += a2.shape[1]

    def build(self):
        return np.ascontiguousarray(np.concatenate(self.items, 1))


def pg(a):
    a = np.asarray(a, np.float32)
    sh = a.shape[2:]
    return np.ascontiguousarray(
        a.reshape(16, 2, 64, *sh).transpose(1, 2, 0, *range(3, 3 + len(sh))).reshape(128, 16, *sh))


def unpg(x):
    return np.asarray(x).reshape(2, 64, 16).transpose(2, 0, 1).reshape(32, 64)


def chunkify(W):
    return np.ascontiguousarray(W.reshape(8, 128, 256).transpose(1, 0, 2)).reshape(128, 2048)


FFN_COLS = np.concatenate([np.concatenate([np.arange(m * 128, (m + 1) * 128),
                                           np.arange(2816 + m * 128, 2816 + (m + 1) * 128)]) for m in range(22)])


def host_shared(inp):
    g = lambda k: np.asarray(inp[k], np.float32)[0]
    W_in = g("w_in")
    win = []
    for c0 in (0, 256):
        win.append(chunkify(W_in[:, c0:c0 + 256]))
    for c0 in range(512, 1536, 256):
        win.append(chunkify(W_in[:, c0:c0 + 256]))
    dtp = np.zeros((1024, 256), np.float32)
    dtp[:, :8] = W_in[:, 1536:1544]
    win.append(chunkify(dtp))
    for c0 in (1544, 1800):
        win.append(chunkify(W_in[:, c0:c0 + 256]))
    W_up = g("w_up")
    for m in range(22):
        win.append(chunkify(W_up[:, FFN_COLS[m * 256:(m + 1) * 256]]))
    wstream = np.ascontiguousarray(np.stack(win))
    W_down = g("w_down")
    wd = np.ascontiguousarray(W_down.reshape(11, 2, 128, 1024).transpose(0, 2, 1, 3).reshape(11, 128, 2048))
    wout = np.ascontiguousarray(g("w_out").reshape(8, 128, 1024).transpose(1, 0, 2).reshape(128, 8192))

    pk = Packer()
    pk.add("ident", np.eye(128))
    s_i = np.arange(128)
    pk.add("negmask", np.where(s_i[None, :] >= s_i[:, None], 0.0, -1e30))
    km = (np.arange(8)[None, :] >= np.arange(8)[:, None]).astype(np.float32)
    pk2 = Packer()
    pk2.add("kmask", np.kron(km, np.ones((16, 16))))
    pk2.add("rm", np.stack([(np.arange(128) < 64), (np.arange(128) >= 64)], 1).astype(np.float32))
    kvec = lambda v, nk: v.reshape(nk, 128).T
    pk.add("wpre_k", kvec(g("pre_mix_norm_w"), 8))
    pk.add("wffn_k", kvec(g("pre_ffn_norm_w"), 8))
    pk.add("wssd_k", kvec(g("ssd_norm_w"), 4))
    pk.add("wpost_bc", np.tile(g("post_mix_norm_w")[None, :], (128, 1)))
    pk.add("wpf_bc", np.tile(g("post_ffn_norm_w")[None, :], (128, 1)))
    pk.add("cw", g("ssd_conv_w").reshape(4, 8, 128).transpose(2, 1, 0))
    pk.add("cb", g("ssd_conv_b").reshape(8, 128).T)
    pk.add("fw", g("ffn_conv_w")[:, FFN_COLS].reshape(3, 44, 128).transpose(2, 1, 0))
    pk.add("fb", g("ffn_conv_b")[FFN_COLS].reshape(44, 128).T)
    pk.add("D_bc", np.tile(g("ssd_d")[None, :], (128, 1)))
    selh = np.zeros((8, 8, 128), np.float32)
    for h in range(8):
        selh[h, h, :] = 1.0
    pk.add("selh", selh)
    pk.add("dtb8", g("ssd_dt_bias").reshape(8, 1))
    pk.add("alog8", g("ssd_a_log").reshape(8, 1))
    pk2.add("lamre", pg(g("s5_lambda_re")))
    pk2.add("lamim", pg(g("s5_lambda_im")))
    pk2.add("logdt", pg(np.repeat(g("s5_log_dt")[:, None], 64, 1)))
    pk2.add("Bre", pg(g("s5_b_re")))
    pk2.add("Bim", pg(g("s5_b_im")))
    pk2.add("Cre", pg(g("s5_c_re").transpose(0, 2, 1)))
    pk2.add("Cim", pg(g("s5_c_im").transpose(0, 2, 1)))
    pk.add("d8", np.tile(g("s5_d"), (1, 8)).T)
    gw = g("s5_glu_w")
    gb = g("s5_glu_b")
    GW = np.zeros((128, 4, 2, 128), np.float32)
    for gg in range(32):
        blk, g8 = gg // 8, gg % 8
        GW[g8 * 16:(g8 + 1) * 16, blk, 0, g8 * 16:(g8 + 1) * 16] = gw[gg][:, :16]
        GW[g8 * 16:(g8 + 1) * 16, blk, 1, g8 * 16:(g8 + 1) * 16] = gw[gg][:, 16:]
    pk2.add("GW", GW)
    pk.add("gba", gb[:, :16].reshape(4, 128).T)
    pk.add("gbb", gb[:, 16:].reshape(4, 128).T)
    return dict(wstream=wstream, wd=wd, wout=wout, cst=pk.build(), cst2=pk2.build()), pk, pk2


def host_core(inp, i, T):
    f = lambda a: np.ascontiguousarray(np.asarray(a, np.float32))
    d = {}
    d["x"] = f(inp["x_prompt"][i][:T])
    d["xs"] = f(inp["x_sample"][i])
    d["c_conv"] = f(np.asarray(inp["cache_ssd_conv"])[0, i].reshape(3, 8, 128).transpose(2, 1, 0))
    d["c_ssd"] = f(np.asarray(inp["state_ssd"])[0, i].reshape(4, 128, 128).transpose(1, 0, 2))
    d["c_s5re"] = f(pg(np.asarray(inp["state_s5_re"])[0, i]))
    d["c_s5im"] = f(pg(np.asarray(inp["state_s5_im"])[0, i]))
    d["c_ffn"] = f(np.asarray(inp["cache_ffn_conv"])[0, i][:, FFN_COLS].reshape(2, 44, 128).transpose(2, 1, 0))
    return d


def build(T, pk, pk2):
    nc = bass.Bass("TRN2", target_bir_lowering=False)
    P = Prog(nc)
    NCST = pk.n

    def din(name, shape, dt=F32):
        return nc.dram_tensor(name, list(shape), dt, kind="ExternalInput").ap()

    def dout(name, shape):
        return nc.dram_tensor(name, list(shape), F32, kind="ExternalOutput").ap()
    x_d = din("x", [T, 1024])
    xs_d = din("xs", [T_SAMPLE, 1024])
    cconv_d = din("c_conv", [128, 8, 3])
    cssd_d = din("c_ssd", [128, 4, 128])
    cs5re_d = din("c_s5re", [128, 16])
    cs5im_d = din("c_s5im", [128, 16])
    cffn_d = din("c_ffn", [128, 44, 2])
    wstream_d = din("wstream", [NWIN + NWUP, 128, 2048])
    wd_d = din("wd", [11, 128, 2048])
    wout_d = din("wout", [128, 8192])
    cst_d = din("cst", [128, NCST])
    cst2_d = din("cst2", [128, pk2.n])
    y_d = dout("y", [T, 1024])
    ys_d = dout("ys", [T_SAMPLE, 1024])
    outs = {}
    for sfx in ("p", "s"):
        outs["conv" + sfx] = dout("o_conv" + sfx, [128, 8, 3])
        outs["ssd" + sfx] = dout("o_ssd" + sfx, [128, 4, 128])
        outs["s5re" + sfx] = dout("o_s5re" + sfx, [128, 16])
        outs["s5im" + sfx] = dout("o_s5im" + sfx, [128, 16])
        outs["ffn" + sfx] = dout("o_ffn" + sfx, [128, 44, 2])
    wsc_d = nc.dram_tensor("wsc", [NWIN + NWUP, 128, 2048], BF16).ap()
    wdsc_d = nc.dram_tensor("wdsc", [11, 128, 2048], BF16).ap()
    woutsc_d = nc.dram_tensor("woutsc", [128, 8192], BF16).ap()

    def sb(name, shape, dt=F32):
        return nc.alloc_sbuf_tensor("sb_" + name, list(shape), dt)

    B = {}

    def bf(n):
        if n not in B:
            B[n] = Buf(n)
        return B[n]

    ARENA_BYTES = 56 * 1024
    arena = sb("arena", [128, ARENA_BYTES // 4])
    arena_addr = nc.lookup_mloc(arena).addr
    sect_off = {}
    arena_items = []

    ar_pos = {}

    def ar(section, name, shape, dt=F32, bufname=None, at=None):
        esz = 4 if dt == F32 else 2
        n = int(np.prod(shape[1:]))
        nb4 = (n * esz + 31) // 32 * 32
        if at is not None:
            off = ar_pos[at]
        else:
            off = sect_off.get(section, 0)
            sect_off[section] = off + nb4
        ar_pos[name] = off
        assert off + nb4 <= ARENA_BYTES, (section, name, off + nb4)
        ap = nc.alloc_sbuf_tensor_at("ar_%s_%s" % (section, name), list(shape), dt, offset=arena_addr + off)
        b = bf(bufname or name)
        for (sec2, o2, s2, b2) in arena_items:
            if sec2 != section and b2 is not b and not (off + nb4 <= o2 or o2 + s2 <= off):
                if b2 not in b.al:
                    b.al.append(b2)
                if b not in b2.al:
                    b2.al.append(b)
        arena_items.append((section, off, nb4, b))
        return ap

    cst = sb("cst", [128, NCST])

    def C(name):
        off, n, shp = pk.off[name]
        ap = cst[:, off:off + n]
        if len(shp) == 2:
            ap = ap.rearrange("p (a b) -> p a b", a=shp[0])
        elif len(shp) == 3:
            ap = ap.rearrange("p (a b c) -> p a b c", a=shp[0], b=shp[1])
        return ap
    pcst = ar("pro", "pcst", [128, pk2.n], bufname="s5c")

    def C2(name):
        off, n, shp = pk2.off[name]
        ap = pcst[:, off:off + n]
        if len(shp) == 2:
            ap = ap.rearrange("p (a b) -> p a b", a=shp[0])
        elif len(shp) == 3:
            ap = ap.rearrange("p (a b c) -> p a b c", a=shp[0], b=shp[1])
        return ap
    identF = C("ident")
    identB = sb("identB", [128, 128], BF16)
    DI = sb("DI", [128, 8, 128], BF16)
    a8 = sb("a8", [8, 1])
    cmask = sb("cmask", [8, NT])
    wout_sb = sb("wout_sb", [128, 8, 1024], BF16)
    GWb = sb("GWb", [128, 4, 2, 128], BF16)
    ZT = sb("ZT", [128, 2, 32, 128], BF16)
    Fg = sb("Fg", [128, 2, 32, 128], BF16)
    Km = sb("Km", [128, 32, 128], BF16)
    JMAX = NT // 8
    cosT = sb("cosT", [128, 16, JMAX])
    sinT = sb("sinT", [128, 16, JMAX])
    r8z = sb("r8z", [128, 16, JMAX])
    r8 = sb("r8", [128, 16])
    xh = sb("xh", [128, 8, 3])
    stT = sb("stT", [128, 512])
    stB = sb("stB", [128, 512], BF16)
    Hc = sb("Hc", [128, 2, 16])
    fh = sb("fh", [128, 44, 2])
    xres = sb("xres", [128, 4, 1024])
    xb16s = [sb("xb16_%d" % i, [128, 1024], BF16) for i in range(2)]
    junk = sb("junk", [128, 1024], BF16)
    stt = sb("stt", [128, 4, 8])
    xnT = sb("xnT", [128, 8, NT], BF16)
    mixinT = sb("mixinT", [128, 8, NT], BF16)
    tk = sb("tk", [128, 4, 16])
    ek = sb("ek", [128, 4, 8])
    wk = sb("wk", [128, 8])
    wk2s = [sb("wk2_%d" % i, [128, 8]) for i in range(2)]
    cks = [sb("ck%d" % i, [128, 8]) for i in range(2)]
    gs = sb("gs", [128, 8])
    ring = [sb("ring%d" % i, [128, 2048], BF16) for i in range(RING)]
    zs = ar("ssd", "zs", [128, 4, 512], BF16)
    ct = [ar("ssd", "ct%d" % i, [128, NT]) for i in range(2)]
    xbw = [ar("ssd", "xbw%d" % i, [128, 3 + NT]) for i in range(2)]
    xbcT = ar("ssd", "xbcT", [128, 8, NT], BF16)
    dtmp = ar("ssd", "dtmp", [8, NT])
    dtT = ar("ssd", "dtT", [8, NT])
    dtaT = ar("ssd", "dtaT", [8, NT])
    csT = ar("ssd", "csT", [8, NT])
    xsb_toks = [ar("ssd", "xsb_tok%d" % i, [128, 6, 128], BF16) for i in range(2)]
    arg = ar("ssd", "arg", [128, 8, 128])
    Eb = ar("ssd", "Eb", [128, 8, 128])
    Mbs = [ar("ssd", "Mb%d" % i, [128, 8, 128], BF16) for i in range(2)]
    ytmp = ar("ssd", "ytmp", [128, 512])
    ytmp2 = ar("ssd", "ytmp2", [128, 512])
    gbuf = ar("ssd", "gbuf", [128, 512])
    gn = ar("ssd", "gn", [128, 512], BF16)
    xw_tok = ar("ssd", "xw_tok", [128, 512], BF16)
    u_tok2 = ar("s5", "u_tok2", [64, 32, 8, 16], BF16)
    U8 = ar("s5", "U8", [128, 32, JMAX], BF16)
    R_ = ar("s5", "R_", [128, 2, 8, JMAX])
    t_a = ar("s5", "t_a", [128, 8, JMAX])
    t_b = ar("s5", "t_b", [128, 8, JMAX])
    G_ = ar("s5", "G_", [128, 2, 8, JMAX])
    Hn = ar("s5", "Hn", [128, 2, 16, JMAX])
    Hprev = ar("s5", "Hprev", [128, 2, 16, JMAX], BF16)
    yD8 = ar("s5", "yD8", [128, 32, JMAX], BF16)
    yj = ar("s5", "yj", [64, 8, 512], BF16)
    ygT = ar("s5", "ygT", [128, 4, NT], BF16)
    sgt = ar("s5", "sgt", [128, NT])
    actT = ar("ffn", "actT", [128, 22, NT], BF16)
    upb = [ar("ffn", "upb%d" % i, [128, 2 + NT]) for i in range(4)]
    fct = [ar("ffn", "fct%d" % i, [128, NT]) for i in range(4)]
    gel = ar("ffn", "gel", [128, NT])
    mixtmps = [ar("ffn", "mixtmp%d" % i, [128, 1024]) for i in range(2)]
    mixtmp = mixtmps[0]
    NB = 6
    pb = [nc.alloc_psum_tensor("pb%d" % i, [128, 512], F32) for i in range(NB)]
    pt = [nc.alloc_psum_tensor("pt%d" % i, [128, 512], F32) for i in range(2)]
    b_pb = [Buf("pb%d" % i) for i in range(NB)]
    b_pt = [Buf("pt%d" % i) for i in range(2)]
    bank_i = [0, 0]

    def nbank():
        i = bank_i[0] % NB
        bank_i[0] += 1
        return pb[i], b_pb[i]

    def ntbank():
        i = bank_i[1] % 2
        bank_i[1] += 1
        return pt[i], b_pt[i]

    PRO = 9
    lvl = [0]

    def pe_transposes(n, src, K, M, src_reads, evac):
        for s0 in range(0, n, 4):
            cnt = min(4, n - s0)
            tb, tbb = ntbank()
            for i in range(cnt):
                op(PE, lambda e, tb=tb, i=i, s_=s0 + i: e.matmul(tb[:M, i * 128:i * 128 + K], lhsT=src(s_), rhs=identB[:K, :K], start=True, stop=True),
                   reads=list(src_reads) + [bf("identB")], writes=[tbb], sig=(i == cnt - 1))
            view = tb[:M, 0:cnt * 128].rearrange("p (a b) -> p a b", a=cnt)[:, :, :K]
            evac(s0, cnt, view, tbb)

    def op(eng, fn, **kw):
        if lvl[0] > PRO:
            return
        P.op(eng, fn, **kw)
    ACT, DVE, PE, POOL, SP = "act", "dve", "pe", "pool", "sp"

    S5 = bf("s5c")
    op(SP, lambda e: e.dma_start(out=cst[:], in_=cst_d), writes=[bf("cst")], dma="ld_cst")
    op(SP, lambda e: e.dma_start(out=pcst[:], in_=cst2_d), writes=[S5], dma="ld_cst2")
    for c in range(NWIN + NWUP):
        op(POOL, lambda e, c=c: e.dma_start(out=wsc_d[c], in_=wstream_d[c], max_dma_last_dim=8192),
           writes=[bf("wsc")], dma="castw_sc")
    for c in range(11):
        op(POOL, lambda e, c=c: e.dma_start(out=wdsc_d[c], in_=wd_d[c], max_dma_last_dim=8192),
           writes=[bf("wdsc")], dma="castw_wd")
    for c in range(4):
        op(POOL, lambda e, c=c: e.dma_start(out=woutsc_d[:, c * 2048:(c + 1) * 2048], in_=wout_d[:, c * 2048:(c + 1) * 2048],
                                            max_dma_last_dim=8192), writes=[bf("woutsc")], dma="castw_out")
    op(SP, lambda e: e.dma_start(out=wout_sb[:].rearrange("p k c -> p (k c)"), in_=woutsc_d), reads=[bf("woutsc")],
       writes=[bf("wout_sb")], dma="ld_wout")
    op(DVE, lambda e: e.tensor_copy(out=identB[:], in_=identF), reads=[bf("cst")], writes=[bf("identB")])
    op(DVE, lambda e: e.tensor_tensor(out=DI[:], in0=identF.unsqueeze(1).to_broadcast([128, 8, 128]),
                                      in1=C("D_bc").unsqueeze(2).to_broadcast([128, 8, 128]), op=ALU.mult),
       reads=[bf("cst")], writes=[bf("DI")])
    op(ACT, lambda e: e.activation(out=a8[:], in_=C("alog8")[0:8, :], func=AF.Exp), reads=[bf("cst")], writes=[bf("a8")])
    op(DVE, lambda e: e.tensor_scalar(out=a8[:], in0=a8[:], scalar1=-1.0, scalar2=None, op0=ALU.mult), reads=[bf("a8")], writes=[bf("a8")])
    op(POOL, lambda e: e.memset(cmask[:], 1.0), writes=[bf("cmask")])
    op(POOL, lambda e: e.memset(cmask[:].rearrange("p (c j) -> p c j", c=4)[:, :, 0:1], 0.0), writes=[bf("cmask")])
    op(DVE, lambda e: e.tensor_copy(out=GWb[:], in_=C2("GW")), reads=[S5], writes=[bf("GWb")])

    lvl[0] = 1
    def s5tile(name, shape=(128, 16)):
        return ar("pro", "s5_" + name, list(shape), bufname="s5c")
    dtb = s5tile("dtb"); a_ = s5tile("a"); th = s5tile("th"); rr = s5tile("rr")
    cc = s5tile("cc"); ss_ = s5tile("ss"); t1 = s5tile("t1"); t2 = s5tile("t2"); t3 = s5tile("t3")
    lbr = s5tile("lbr"); lbi = s5tile("lbi"); den = s5tile("den"); qre = s5tile("qre"); qim = s5tile("qim")
    Bbre = s5tile("Bbre", (128, 16, 16)); Bbim = s5tile("Bbim", (128, 16, 16))
    u1 = s5tile("u1", (128, 16, 16)); u2 = s5tile("u2", (128, 16, 16))
    Pr = s5tile("Pr", (128, 9, 16)); Pi = s5tile("Pi", (128, 9, 16))
    Qr = s5tile("Qr", (128, 8, 16)); Qi = s5tile("Qi", (128, 8, 16))
    def s5tile_b(name, shape):
        return ar("pro", "s5_" + name, list(shape), BF16, bufname="s5c")
    Zre = s5tile_b("Zre", (128, 16, 8, 16)); Zim = s5tile_b("Zim", (128, 16, 8, 16))
    Ykre = s5tile_b("Ykre", (128, 16, 8, 16)); Ykim = s5tile_b("Ykim", (128, 16, 8, 16))
    Fm = ar("pro", "Fm", [128, 2, 16, 128], BF16, bufname="s5c", at="s5_Ykre")
    halfpi = s5tile("halfpi", (128, 1))

    def dv(fn, r=(S5, ), w=(S5, )):
        op(DVE, fn, reads=list(r) + [bf("cst")], writes=list(w))

    def ac(fn, r=(S5, ), w=(S5, )):
        op(ACT, fn, reads=list(r) + [bf("cst")], writes=list(w))
    TT = lambda o, a, b, o_: (lambda e: e.tensor_tensor(out=o, in0=a, in1=b, op=o_))
    op(POOL, lambda e: e.memset(halfpi[:], float(np.pi / 2)), writes=[S5])
    ac(lambda e: e.activation(out=dtb[:], in_=C2("logdt"), func=AF.Exp))
    dv(TT(a_[:], C2("lamre"), dtb[:], ALU.mult))
    dv(TT(th[:], C2("lamim"), dtb[:], ALU.mult))
    ac(lambda e: e.activation(out=rr[:], in_=a_[:], func=AF.Exp))
    import math
    dv(lambda e: e.tensor_scalar(out=t1[:], in0=th[:], scalar1=1.0 / 16, scalar2=None, op0=ALU.mult))
    dv(TT(t2[:], t1[:], t1[:], ALU.mult))
    sc = [(-1.0) ** k / math.factorial(2 * k + 1) for k in range(7)]
    dv(lambda e: e.tensor_scalar(out=t3[:], in0=t2[:], scalar1=sc[6], scalar2=None, op0=ALU.mult))
    for k in (5, 4, 3, 2, 1):
        dv(lambda e, k=k: e.scalar_tensor_tensor(out=t3[:], in0=t3[:], scalar=sc[k], in1=t2[:], op0=ALU.add, op1=ALU.mult))
    dv(lambda e: e.scalar_tensor_tensor(out=ss_[:], in0=t3[:], scalar=1.0, in1=t1[:], op0=ALU.add, op1=ALU.mult))
    cs_ = [(-1.0) ** k / math.factorial(2 * k) for k in range(8)]
    dv(lambda e: e.tensor_scalar(out=t3[:], in0=t2[:], scalar1=cs_[7], scalar2=None, op0=ALU.mult))
    for k in (6, 5, 4, 3, 2, 1):
        dv(lambda e, k=k: e.scalar_tensor_tensor(out=t3[:], in0=t3[:], scalar=cs_[k], in1=t2[:], op0=ALU.add, op1=ALU.mult))
    dv(lambda e: e.tensor_scalar(out=cc[:], in0=t3[:], scalar1=1.0, scalar2=None, op0=ALU.add))
    for _ in range(4):
        dv(TT(t1[:], cc[:], cc[:], ALU.mult))
        dv(TT(t2[:], ss_[:], ss_[:], ALU.mult))
        dv(TT(t3[:], cc[:], ss_[:], ALU.mult))
        dv(TT(cc[:], t1[:], t2[:], ALU.subtract))
        dv(lambda e: e.tensor_scalar(out=ss_[:], in0=t3[:], scalar1=2.0, scalar2=None, op0=ALU.mult))
    dv(TT(lbr[:], rr[:], cc[:], ALU.mult))
    dv(TT(lbi[:], rr[:], ss_[:], ALU.mult))
    dv(TT(t1[:], C2("lamre"), C2("lamre"), ALU.mult))
    dv(TT(t2[:], C2("lamim"), C2("lamim"), ALU.mult))
    dv(TT(den[:], t1[:], t2[:], ALU.add))
    dv(lambda e: e.reciprocal(out=den[:], in_=den[:]))
    dv(lambda e: e.tensor_scalar(out=t3[:], in0=lbr[:], scalar1=-1.0, scalar2=None, op0=ALU.add))
    dv(TT(t1[:], t3[:], C2("lamre"), ALU.mult))
    dv(TT(t2[:], lbi[:], C2("lamim"), ALU.mult))
    dv(TT(t1[:], t1[:], t2[:], ALU.add))
    dv(TT(qre[:], t1[:], den[:], ALU.mult))
    dv(TT(t1[:], lbi[:], C2("lamre"), ALU.mult))
    dv(TT(t2[:], t3[:], C2("lamim"), ALU.mult))
    dv(TT(t1[:], t1[:], t2[:], ALU.subtract))
    dv(TT(qim[:], t1[:], den[:], ALU.mult))
    bc16 = lambda ap: ap.unsqueeze(2).to_broadcast([128, 16, 16])
    dv(TT(u1[:], C2("Bre"), bc16(qre[:]), ALU.mult))
    dv(TT(u2[:], C2("Bim"), bc16(qim[:]), ALU.mult))
    dv(TT(Bbre[:], u1[:], u2[:], ALU.subtract))
    dv(TT(u1[:], C2("Bim"), bc16(qre[:]), ALU.mult))
    dv(TT(u2[:], C2("Bre"), bc16(qim[:]), ALU.mult))
    dv(TT(Bbim[:], u1[:], u2[:], ALU.add))
    op(POOL, lambda e: e.memset(Pr[:, 0, :], 1.0), writes=[S5])
    op(POOL, lambda e: e.memset(Pi[:, 0, :], 0.0), writes=[S5])
    op(POOL, lambda e: e.memset(Qr[:, 0, :], 1.0), writes=[S5])
    op(POOL, lambda e: e.memset(Qi[:, 0, :], 0.0), writes=[S5])

    T1, T2, T3, U1, U2, OUT = bf("s5_T1"), bf("s5_T2"), bf("s5_T3"), bf("s5_U1"), bf("s5_U2"), bf("s5_OUT")
    PB = lambda nm, k: bf("s5_%s%d" % (nm, k))
    fine = [T1, T2, T3, U1, U2, OUT]

    def cmul(outr, outi, ar, ai, br, bi, RI, II, RO, IO):
        dv(TT(t1[:], ar, br, ALU.mult), r=(S5, RI), w=(T1, ))
        dv(TT(t2[:], ai, bi, ALU.mult), r=(S5, II), w=(T2, ))
        dv(TT(t3[:], ar, bi, ALU.mult), r=(S5, RI), w=(T3, ))
        dv(TT(outr, t1[:], t2[:], ALU.subtract), r=(T1, T2), w=(RO, ))
        dv(TT(t1[:], ai, br, ALU.mult), r=(S5, II), w=(T1, ))
        dv(TT(outi, t3[:], t1[:], ALU.add), r=(T3, T1), w=(IO, ))
        fine.extend([RO, IO])
    for k in range(8):
        cmul(Pr[:, k + 1, :], Pi[:, k + 1, :], Pr[:, k, :], Pi[:, k, :], lbr[:], lbi[:], PB("Pr", k), PB("Pi", k), PB("Pr", k + 1), PB("Pi", k + 1))
    ivr = s5tile("ivr"); ivi = s5tile("ivi")
    dv(TT(t1[:], lbr[:], lbr[:], ALU.mult))
    dv(TT(t2[:], lbi[:], lbi[:], ALU.mult))
    dv(TT(t1[:], t1[:], t2[:], ALU.add))
    dv(lambda e: e.reciprocal(out=t1[:], in_=t1[:]))
    dv(TT(ivr[:], lbr[:], t1[:], ALU.mult))
    dv(TT(ivi[:], lbi[:], t1[:], ALU.mult))
    dv(lambda e: e.tensor_scalar(out=ivi[:], in0=ivi[:], scalar1=-1.0, scalar2=None, op0=ALU.mult))
    for k in range(7):
        cmul(Qr[:, k + 1, :], Qi[:, k + 1, :], Qr[:, k, :], Qi[:, k, :], ivr[:], ivi[:], PB("Qr", k), PB("Qi", k), PB("Qr", k + 1), PB("Qi", k + 1))

    def cmul_bc(outr, outi, pr, pi, xr, xi, PRB, PIB, neg_im=False):
        dv(TT(u1[:], xr, bc16(pr), ALU.mult), r=(S5, PRB), w=(U1, ))
        dv(TT(u2[:], xi, bc16(pi), ALU.mult), r=(S5, PIB), w=(U2, ))
        dv(TT(outr, u1[:], u2[:], ALU.subtract), r=(U1, U2), w=(OUT, ))
        dv(TT(u1[:], xi, bc16(pr), ALU.mult), r=(S5, PRB), w=(U1, ))
        dv(TT(u2[:], xr, bc16(pi), ALU.mult), r=(S5, PIB), w=(U2, ))
        if neg_im:
            dv(lambda e: e.scalar_tensor_tensor(out=outi, in0=u1[:], scalar=-1.0, in1=u2[:], op0=ALU.mult, op1=ALU.subtract), r=(U1, U2), w=(OUT, ))
        else:
            dv(TT(outi, u1[:], u2[:], ALU.add), r=(U1, U2), w=(OUT, ))
    for s8 in range(8):
        cmul_bc(Zre[:, :, s8, :], Zim[:, :, s8, :], Pr[:, 7 - s8, :], Pi[:, 7 - s8, :], Bbre[:], Bbim[:], PB("Pr", 7 - s8), PB("Pi", 7 - s8))
    lvl[0] = 2
    Fm4 = [Fm[:, ri, :, :].rearrange("p g (t c) -> p g t c", t=8) for ri in range(2)]
    for t8 in range(8):
        cmul_bc(Fm4[0][:, :, t8, :], Fm4[1][:, :, t8, :], Pr[:, t8 + 1, :], Pi[:, t8 + 1, :], C2("Cre"), C2("Cim"), PB("Pr", t8 + 1), PB("Pi", t8 + 1), neg_im=True)
    for ri in range(2):
        for gl in range(2):
            dv(lambda e, ri=ri, gl=gl: e.tensor_scalar(out=Fg[:, ri, gl:32:2, :], in0=Fm[:, ri, :, :], scalar1=C2("rm")[:, gl:gl + 1], scalar2=None, op0=ALU.mult), r=(S5, OUT), w=(S5, bf("Fm")))
    for t8 in range(8):
        cmul_bc(Ykre[:, :, t8, :], Ykim[:, :, t8, :], Qr[:, 7 - t8, :], Qi[:, 7 - t8, :], C2("Cre"), C2("Cim"), PB("Qr", 7 - t8), PB("Qi", 7 - t8), neg_im=True)
    dv(lambda e: e.tensor_copy(out=t1[:, 0:1], in_=t1[:, 0:1]), r=tuple([S5] + fine), w=(S5, T1))
    lvl[0] = 3
    Zre2 = Zre[:].rearrange("p g s c -> p g (s c)")
    Zim2 = Zim[:].rearrange("p g s c -> p g (s c)")
    Ykre2 = Ykre[:].rearrange("p g s c -> p g (s c)")
    Ykim2 = Ykim[:].rearrange("p g s c -> p g (s c)")
    op(POOL, lambda e: e.memset(ZT[:].rearrange("p a g q -> p (a g q)"), 0.0), writes=[bf("ZT")])
    for ri, Z2 in enumerate((Zre2, Zim2)):
        def ev_zt(s0, cnt, view, tbb, ri=ri):
            op(ACT, lambda e: e.activation(out=ZT[:, ri, 2 * s0:2 * s0 + 2 * cnt:2, 0:64], in_=view[:, :, 0:64], func=AF.Copy), reads=[tbb], writes=[bf("ZT")])
            op(ACT, lambda e: e.activation(out=ZT[:, ri, 2 * s0 + 1:2 * s0 + 2 * cnt:2, 64:128], in_=view[:, :, 64:128], func=AF.Copy), reads=[tbb], writes=[bf("ZT")])
        pe_transposes(16, lambda gp, Z2=Z2: Z2[:, gp, :], 128, 128, [S5], ev_zt)
    lvl[0] = 4
    Zm = [[s5tile_b("Zm%d%d" % (ri, gl), (128, 16, 128)) for gl in range(2)] for ri in range(2)]
    for ri, Z2 in enumerate((Zre2, Zim2)):
        for gl in range(2):
            dv(lambda e, ri=ri, gl=gl, Z2=Z2: e.tensor_scalar(out=Zm[ri][gl][:], in0=Z2, scalar1=C2("rm")[:, gl:gl + 1], scalar2=None, op0=ALU.mult))
    for g0 in range(0, 32, 4):
        bk, bb_ = nbank()
        for gi in range(4):
            gg = g0 + gi
            gp, gl = gg // 2, gg % 2
            op(PE, lambda e, bk=bk, gi=gi, gp=gp, gl=gl: e.matmul(bk[:, gi * 128:(gi + 1) * 128], lhsT=Zm[0][gl][:, gp, :], rhs=Ykre2[:, gp, :], start=True, stop=False),
               reads=[S5], writes=[bb_], sig=False)
            op(PE, lambda e, bk=bk, gi=gi, gp=gp, gl=gl: e.matmul(bk[:, gi * 128:(gi + 1) * 128], lhsT=Zm[1][gl][:, gp, :], rhs=Ykim2[:, gp, :], start=False, stop=True),
               reads=[S5], writes=[bb_], sig=(gi == 3))
        op(DVE, lambda e, bk=bk, g0=g0: e.tensor_tensor(out=Km[:, g0:g0 + 4, :], in0=bk[:].rearrange("p (g j) -> p g j", g=4),
                                                         in1=C2("kmask").unsqueeze(1).to_broadcast([128, 4, 128]), op=ALU.mult),
           reads=[bb_, bf("cst")], writes=[bf("Km")])
        for gi in range(4):
            op(DVE, lambda e, gg=g0 + gi: e.scalar_tensor_tensor(out=Km[:, gg, :], in0=identF, scalar=C("d8")[:, gg:gg + 1], in1=Km[:, gg, :], op0=ALU.mult, op1=ALU.add),
               reads=[bf("Km"), bf("cst")], writes=[bf("Km")])
    lvl[0] = 5
    c8 = s5tile("c8"); s8t = s5tile("s8t")
    ac(lambda e: e.activation(out=r8[:], in_=a_[:], func=AF.Exp, scale=8.0))
    dv(lambda e: e.reciprocal(out=t1[:], in_=r8[:]))
    dv(TT(c8[:], Pr[:, 8, :], t1[:], ALU.mult))
    dv(TT(s8t[:], Pi[:, 8, :], t1[:], ALU.mult))
    dv(lambda e: e.tensor_copy(out=cosT[:, :, 0], in_=c8[:]))
    dv(lambda e: e.tensor_copy(out=sinT[:, :, 0], in_=s8t[:]))
    w1 = ar("pro", "s5_w1", [128, 16, JMAX // 2], F32, bufname="s5c", at="s5_Zre"); w2 = ar("pro", "s5_w2", [128, 16, JMAX // 2], F32, bufname="s5c", at="s5_Zim")
    n = 1
    while n < JMAX:
        bcn = lambda ap, n=n: ap.unsqueeze(2).to_broadcast([128, 16, n])
        cr = cosT[:, :, n - 1]; sr = sinT[:, :, n - 1]
        dv(TT(w1[:, :, :n], cosT[:, :, 0:n], bcn(cr), ALU.mult))
        dv(TT(w2[:, :, :n], sinT[:, :, 0:n], bcn(sr), ALU.mult))
        dv(TT(cosT[:, :, n:2 * n], w1[:, :, :n], w2[:, :, :n], ALU.subtract))
        dv(TT(w1[:, :, :n], cosT[:, :, 0:n], bcn(sr), ALU.mult))
        dv(TT(w2[:, :, :n], sinT[:, :, 0:n], bcn(cr), ALU.mult))
        dv(TT(sinT[:, :, n:2 * n], w1[:, :, :n], w2[:, :, :n], ALU.add))
        n *= 2
    dv(lambda e: e.tensor_copy(out=r8z[:], in_=r8[:].unsqueeze(2).to_broadcast([128, 16, JMAX])))
    op(POOL, lambda e: e.memset(r8z[:, :, 0:1], 0.0), reads=[S5], writes=[S5])
    dv(lambda e: e.tensor_copy(out=r8[:, 0:1], in_=r8[:, 0:1]), r=(S5, bf("ZT"), bf("Km")), w=(S5, bf("Fm"), bf("ZT"), bf("Km")))
    lvl[0] = 0
    S5C = [S5, bf("ZT"), bf("Fm"), bf("Km")]
    if DBG:
        def dump(name, ap, shape):
            d = dout("dbg_" + name, shape)
            op(POOL, lambda e: e.dma_start(out=d, in_=ap), reads=S5C, writes=[bf("dbgout")], dma="dbg_" + name)
        dump("lbr", lbr[:], [128, 16]); dump("lbi", lbi[:], [128, 16]); dump("qre", qre[:], [128, 16]); dump("qim", qim[:], [128, 16])
        dump("Pr", Pr[:], [128, 9, 16]); dump("Pi", Pi[:], [128, 9, 16]); dump("Qr", Qr[:], [128, 8, 16]); dump("Qi", Qi[:], [128, 8, 16])
        dump("cosT", cosT[:], [128, 16, JMAX]); dump("sinT", sinT[:], [128, 16, JMAX]); dump("r8", r8[:], [128, 16]); dump("r8z", r8z[:], [128, 16, JMAX])
        dump("Km", Km[:], [128, 32, 128]); dump("ZT", ZT[:], [128, 2, 32, 128]); dump("Fg", Fg[:], [128, 2, 32, 128])
        dump("Bbre", Bbre[:], [128, 16, 16]); dump("Zre", Zre[:], [128, 16, 8, 16]); dump("Ykre", Ykre[:], [128, 16, 8, 16])

    ring_i = [0]
    b_ring = [Buf("ring%d" % i) for i in range(RING)]

    def wload(src_ap, srcbuf):
        i = ring_i[0] % RING
        ring_i[0] += 1
        op(SP, lambda e, i=i, src_ap=src_ap: e.dma_start(out=ring[i][:], in_=src_ap), reads=[srcbuf], writes=[b_ring[i]], dma=("ring", i))
        return ring[i], b_ring[i]

    XR = [bf("xres%d" % b) for b in range(4)]

    def stage_rstd(blocks, npart):
        for b in blocks:
            op(DVE, lambda e, b=b: e.tensor_scalar(out=stt[:npart, b, 1:2], in0=stt[:npart, b, 0:1], scalar1=1.0 / 1024, scalar2=EPS, op0=ALU.mult, op1=ALU.add),
               reads=[bf("stt")], writes=[bf("stt")])
        for b in blocks:
            op(ACT, lambda e, b=b: e.activation(out=stt[:npart, b, 2:3], in_=stt[:npart, b, 1:2], func=AF.Ln), reads=[bf("stt")], writes=[bf("stt")])
        for b in blocks:
            op(ACT, lambda e, b=b: e.activation(out=stt[:npart, b, 3:4], in_=stt[:npart, b, 2:3], func=AF.Exp, scale=-0.5), reads=[bf("stt")], writes=[bf("stt")])

    def norm_transpose_all(blocks, npart, dstT, dstbuf, wk_name):
        for b in blocks:
            op(ACT, lambda e, b=b: e.activation(out=junk[:npart, :], in_=xres[:npart, b, :], func=AF.Square, accum_out=stt[:npart, b, 0:1]),
               reads=[XR[b]], writes=[bf("junk"), bf("stt")])
        stage_rstd(blocks, npart)
        for b in blocks:
            xb = xb16s[b % 2]
            XB_ = bf("xb16_%d" % (b % 2))
            tcols = slice(b * 128, b * 128 + npart)
            op(ACT, lambda e, b=b, xb=xb: e.activation(out=xb[:npart, :], in_=xres[:npart, b, :], func=AF.Copy, scale=stt[:npart, b, 3:4]),
               reads=[XR[b], bf("stt")], writes=[XB_])

            def ev(s0, cnt, view, tbb, tcols=tcols):
                op(DVE, lambda e: e.tensor_tensor(out=dstT[:, s0:s0 + cnt, tcols], in0=view, in1=C(wk_name)[:, s0:s0 + cnt].unsqueeze(2).to_broadcast([128, cnt, npart]), op=ALU.mult),
                   reads=[tbb, bf("cst")], writes=[dstbuf])
            pe_transposes(8, lambda k, xb=xb: xb[:npart, k * 128:(k + 1) * 128], npart, 128, [XB_], ev)

    def run_tile(xsrc, ydst, nt, first, seq):
        nb = max(1, nt // 128)
        ch = min(128, nt)
        J = nt // 8
        if nt >= 128:
            for b in range(nb):
                op(POOL, lambda e, b=b: e.dma_start(out=xres[:, b, :], in_=xsrc[b * 128:(b + 1) * 128, :]), writes=[XR[b]], dma=("ld_x", b))
        else:
            op(POOL, lambda e: e.dma_start(out=xres[:nt, 0, :], in_=xsrc), writes=XR, dma=("ld_x", 0))
        if STAGE < 1:
            return
        norm_transpose_all(list(range(nb)), ch, xnT, bf("xnT"), "wpre_k")
        wz = [wload(wsc_d[c], bf("wsc")) for c in (0, 1)]
        for b in range(nb):
            bk, bb_ = nbank()
            for hf in range(2):
                for k in range(8):
                    op(PE, lambda e, b=b, hf=hf, k=k, bk=bk: e.matmul(bk[:ch, hf * 256:(hf + 1) * 256], lhsT=xnT[:, k, b * 128:b * 128 + ch],
                                                                      rhs=wz[hf][0][:, k * 256:(k + 1) * 256], start=(k == 0), stop=(k == 7)),
                       reads=[bf("xnT"), wz[hf][1]], writes=[bb_], sig=(k == 7 and hf == 1))
            op(ACT, lambda e, b=b, bk=bk: e.activation(out=zs[:ch, b, :], in_=bk[:ch, :], func=AF.Silu), reads=[bb_], writes=[bf("zs")])
        for c in range(4):
            wr, wrb = wload(wsc_d[2 + c], bf("wsc"))
            for j in range(2):
                blk = 2 * c + j
                bk, bb_ = nbank()
                for k in range(8):
                    op(PE, lambda e, k=k, j=j, bk=bk, wr=wr: e.matmul(bk[:, :nt], lhsT=wr[:, k * 256 + j * 128:k * 256 + (j + 1) * 128], rhs=xnT[:, k, :nt],
                                                                      start=(k == 0), stop=(k == 7)),
                       reads=[bf("xnT"), wrb], writes=[bb_], sig=(k == 7))
                wi = blk % 2
                XB = bf("xbw%d" % wi)
                xw_ = xbw[wi]
                op(POOL, lambda e, blk=blk, xw_=xw_: e.tensor_copy(out=xw_[:, 0:3], in_=xh[:, blk, :]), reads=[bf("xh")], writes=[XB])
                op(ACT, lambda e, xw_=xw_, bk=bk: e.activation(out=xw_[:, 3:3 + nt], in_=bk[:, :nt], func=AF.Copy), reads=[bb_], writes=[XB])
                op(POOL, lambda e, blk=blk, xw_=xw_: e.tensor_copy(out=xh[:, blk, :], in_=xw_[:, nt:nt + 3]), reads=[XB], writes=[bf("xh")])
                cwv = C("cw")
                op(ACT, lambda e, blk=blk, xw_=xw_: e.activation(out=ct[0][:, :nt], in_=xw_[:, 0:nt], func=AF.Identity, scale=cwv[:, blk, 0:1], bias=C("cb")[:, blk:blk + 1]),
                   reads=[XB, bf("cst")], writes=[bf("ct0")])
                op(DVE, lambda e, blk=blk, xw_=xw_: e.scalar_tensor_tensor(out=ct[1][:, :nt], in0=xw_[:, 1:1 + nt], scalar=cwv[:, blk, 1:2], in1=ct[0][:, :nt], op0=ALU.mult, op1=ALU.add),
                   reads=[XB, bf("ct0")], writes=[bf("ct1")])
                op(DVE, lambda e, blk=blk, xw_=xw_: e.scalar_tensor_tensor(out=ct[0][:, :nt], in0=xw_[:, 2:2 + nt], scalar=cwv[:, blk, 2:3], in1=ct[1][:, :nt], op0=ALU.mult, op1=ALU.add),
                   reads=[XB, bf("ct1")], writes=[bf("ct0")])
                op(DVE, lambda e, blk=blk, bk=bk: e.scalar_tensor_tensor(out=ct[1][:, :nt], in0=bk[:, :nt], scalar=cwv[:, blk, 3:4], in1=ct[0][:, :nt], op0=ALU.mult, op1=ALU.add),
                   reads=[bb_, bf("ct0")], writes=[bf("ct1")])
                op(ACT, lambda e, blk=blk: e.activation(out=xbcT[:, blk, :nt], in_=ct[1][:, :nt], func=AF.Silu), reads=[bf("ct1")], writes=[bf("xbcT")])
        wr, wrb = wload(wsc_d[6], bf("wsc"))
        bk, bb_ = nbank()
        for k in range(8):
            op(PE, lambda e, k=k, bk=bk, wr=wr: e.matmul(bk[0:8, :nt], lhsT=wr[:, k * 256:k * 256 + 8], rhs=xnT[:, k, :nt], start=(k == 0), stop=(k == 7)),
               reads=[bf("xnT"), wrb], writes=[bb_], sig=(k == 7))
        op(ACT, lambda e, bk=bk: e.activation(out=dtmp[:, :nt], in_=bk[0:8, :nt], func=AF.Exp, bias=C("dtb8")[0:8, :]), reads=[bb_, bf("cst")], writes=[bf("dtmp")])
        op(ACT, lambda e: e.activation(out=dtT[:, :nt], in_=dtmp[:, :nt], func=AF.Ln, bias=1.0), reads=[bf("dtmp")], writes=[bf("dtT")])
        op(DVE, lambda e: e.tensor_scalar(out=dtaT[:, :nt], in0=dtT[:, :nt], scalar1=a8[:, 0:1], scalar2=None, op0=ALU.mult), reads=[bf("dtT"), bf("a8")], writes=[bf("dtaT")])
        op(DVE, lambda e: e.tensor_tensor_scan(out=csT[:, :nt], data0=cmask[:, :nt], data1=dtaT[:, :nt], initial=0.0, op0=ALU.mult, op1=ALU.add),
           reads=[bf("dtaT"), bf("cmask")], writes=[bf("csT")])
        bk, bb_ = nbank()
        for c in range(nb):
            cols = slice(c * 128, c * 128 + ch)
            op(PE, lambda e, c=c, cols=cols, bk=bk: e.transpose(out=bk[:ch, c * 16:c * 16 + 8], in_=dtT[0:8, cols], identity=identF[0:8, 0:8]),
               reads=[bf("dtT"), bf("cst")], writes=[bb_], sig=False)
            op(PE, lambda e, c=c, cols=cols, bk=bk: e.transpose(out=bk[:ch, c * 16 + 8:c * 16 + 16], in_=csT[0:8, cols], identity=identF[0:8, 0:8]),
               reads=[bf("csT"), bf("cst")], writes=[bb_], sig=(c == nb - 1))
        op(DVE, lambda e, bk=bk: e.tensor_copy(out=tk[:ch, 0:nb, :], in_=bk[:ch, 0:nb * 16].rearrange("p (c x) -> p c x", c=nb)), reads=[bb_], writes=[bf("tk")])
        op(ACT, lambda e: e.activation(out=ek[:ch, 0:nb, :], in_=tk[:ch, 0:nb, 8:16], func=AF.Exp), reads=[bf("tk")], writes=[bf("ek")])
        if STAGE < 2:
            return
        def ssd_front(c):
            cols = slice(c * 128, c * 128 + ch)
            par = c % 2
            xsb_tok, Mb, wk2, ck = xsb_toks[par], Mbs[par], wk2s[par], cks[par]
            XSB, MBB, WK2, CKB = bf("xsb_tok%d" % par), bf("Mb%d" % par), bf("wk2_%d" % par), bf("ck%d" % par)
            def ev_x(s0, cnt, view, tbb):
                op(DVE, lambda e: e.tensor_copy(out=xsb_tok[:ch, s0:s0 + cnt, :], in_=view), reads=[tbb], writes=[XSB])
            pe_transposes(6, lambda j, cols=cols: xbcT[:, j, cols], 128, ch, [bf("xbcT")], ev_x)
            bS, bSb = nbank()
            for g in range(2):
                op(PE, lambda e, g=g, cols=cols, bS=bS: e.matmul(bS[:ch, g * 128:g * 128 + ch], lhsT=xbcT[:, 4 + g, cols], rhs=xbcT[:, 6 + g, cols], start=True, stop=True),
                   reads=[bf("xbcT")], writes=[bSb], sig=(g == 1))
            bR = [nbank(), nbank()]
            for h in range(8):
                op(PE, lambda e, bR=bR, h=h, cols=cols: e.matmul(bR[h // 4][0][:, (h % 4) * 128:(h % 4) * 128 + ch], lhsT=C("selh")[0:8, h, :], rhs=csT[0:8, cols], start=True, stop=True),
                   reads=[bf("csT"), bf("cst")], writes=[bR[h // 4][1]], sig=(h % 4 == 3))
            for h in range(8):
                op(DVE, lambda e, bR=bR, h=h, c=c: e.scalar_tensor_tensor(out=arg[:ch, h, :ch], in0=bR[h // 4][0][:ch, (h % 4) * 128:(h % 4) * 128 + ch], scalar=tk[:ch, c, 8 + h:9 + h],
                                                                   in1=C("negmask")[:ch, :ch], op0=ALU.subtract, op1=ALU.add),
                   reads=[bR[h // 4][1], bf("tk"), bf("cst")], writes=[bf("arg")])
            for hh in range(2):
                op(DVE, lambda e, bR=bR, hh=hh, c=c: e.tensor_tensor(out=wk[:ch, hh * 4:hh * 4 + 4], in0=bR[hh][0][:ch, :].rearrange("p (h l) -> p h l", h=4)[:, :, ch - 1],
                                                              in1=tk[:ch, c, 8 + hh * 4:12 + hh * 4], op=ALU.subtract),
                   reads=[bR[hh][1], bf("tk")], writes=[bf("wk")])
                op(ACT, lambda e, bR=bR, hh=hh: e.activation(out=ck[:, hh * 4:hh * 4 + 4], in_=bR[hh][0][:, :].rearrange("p (h l) -> p h l", h=4)[:, :, ch - 1], func=AF.Exp),
                   reads=[bR[hh][1]], writes=[CKB])
            op(ACT, lambda e: e.activation(out=wk2[:ch, :], in_=wk[:ch, :], func=AF.Exp), reads=[bf("wk")], writes=[WK2])
            op(DVE, lambda e, c=c: e.tensor_tensor(out=wk2[:ch, :], in0=wk2[:ch, :], in1=tk[:ch, c, 0:8], op=ALU.mult), reads=[WK2, bf("tk")], writes=[WK2])
            op(ACT, lambda e: e.activation(out=Eb[:ch, :, :ch], in_=arg[:ch, :, :ch], func=AF.Exp), reads=[bf("arg")], writes=[bf("Eb")])
            for h in range(8):
                g = h // 4
                op(DVE, lambda e, h=h, g=g, c=c, bS=bS: e.scalar_tensor_tensor(out=Mb[:ch, h, :ch], in0=Eb[:ch, h, :ch], scalar=tk[:ch, c, h:h + 1],
                                                                               in1=bS[:ch, g * 128:g * 128 + ch], op0=ALU.mult, op1=ALU.mult),
                   reads=[bf("Eb"), bf("tk"), bSb], writes=[MBB])

        def ssd_back(c):
            cols = slice(c * 128, c * 128 + ch)
            par = c % 2
            xsb_tok, Mb, wk2, ck = xsb_toks[par], Mbs[par], wk2s[par], cks[par]
            XSB, MBB, WK2, CKB = bf("xsb_tok%d" % par), bf("Mb%d" % par), bf("wk2_%d" % par), bf("ck%d" % par)
            bY, bYb = nbank()
            for h in range(8):
                op(PE, lambda e, h=h, bY=bY: e.matmul(bY[:ch, h * 64:(h + 1) * 64], lhsT=DI[:ch, h, :ch], rhs=xsb_tok[:ch, h // 2, (h % 2) * 64:(h % 2) * 64 + 64], start=True, stop=False),
                   reads=[bf("DI"), XSB], writes=[bYb], sig=False)
                op(PE, lambda e, h=h, bY=bY: e.matmul(bY[:ch, h * 64:(h + 1) * 64], lhsT=Mb[:ch, h, :ch], rhs=xsb_tok[:ch, h // 2, (h % 2) * 64:(h % 2) * 64 + 64], start=False, stop=True),
                   reads=[MBB, XSB], writes=[bYb], sig=(h == 7))
            bO, bOb = nbank()
            for g in range(2):
                op(PE, lambda e, g=g, cols=cols, bO=bO: e.matmul(bO[:ch, g * 256:(g + 1) * 256], lhsT=xbcT[:, 6 + g, cols], rhs=stB[:, g * 256:(g + 1) * 256], start=True, stop=True),
                   reads=[bf("xbcT"), bf("stB")], writes=[bOb], sig=(g == 1))
            op(DVE, lambda e, c=c, bO=bO: e.tensor_tensor(out=ytmp[:ch, :].rearrange("p (h x) -> p h x", h=8), in0=bO[:ch, :].rearrange("p (h x) -> p h x", h=8),
                                                          in1=ek[:ch, c, :].unsqueeze(2).to_broadcast([ch, 8, 64]), op=ALU.mult),
               reads=[bOb, bf("ek")], writes=[bf("ytmp")])
            op(DVE, lambda e, bY=bY: e.tensor_tensor(out=ytmp2[:ch, :], in0=ytmp[:ch, :], in1=bY[:ch, :], op=ALU.add), reads=[bf("ytmp"), bYb], writes=[bf("ytmp2")])
            op(DVE, lambda e, c=c: e.tensor_tensor(out=gbuf[:ch, :], in0=ytmp2[:ch, :], in1=zs[:ch, c, :], op=ALU.mult), reads=[bf("ytmp2"), bf("zs")], writes=[bf("gbuf")])
            for g in range(2):
                op(ACT, lambda e, g=g: e.activation(out=junk[:ch, g * 256:(g + 1) * 256], in_=gbuf[:ch, g * 256:(g + 1) * 256], func=AF.Square, accum_out=gs[:ch, g:g + 1]),
                   reads=[bf("gbuf")], writes=[bf("junk"), bf("gs")])
            op(DVE, lambda e: e.tensor_scalar(out=gs[:ch, 2:4], in0=gs[:ch, 0:2], scalar1=1.0 / 256, scalar2=EPS, op0=ALU.mult, op1=ALU.add), reads=[bf("gs")], writes=[bf("gs")])
            op(ACT, lambda e: e.activation(out=gs[:ch, 4:6], in_=gs[:ch, 2:4], func=AF.Ln), reads=[bf("gs")], writes=[bf("gs")])
            op(ACT, lambda e: e.activation(out=gs[:ch, 6:8], in_=gs[:ch, 4:6], func=AF.Exp, scale=-0.5), reads=[bf("gs")], writes=[bf("gs")])
            for g in range(2):
                op(ACT, lambda e, g=g: e.activation(out=gn[:ch, g * 256:(g + 1) * 256], in_=gbuf[:ch, g * 256:(g + 1) * 256], func=AF.Copy, scale=gs[:ch, 6 + g:7 + g]),
                   reads=[bf("gbuf"), bf("gs")], writes=[bf("gn")])
            def ev_g(s0, cnt, view, tbb, cols=cols):
                op(DVE, lambda e: e.tensor_tensor(out=mixinT[:, 0:4, cols], in0=view, in1=C("wssd_k").unsqueeze(2).to_broadcast([128, 4, ch]), op=ALU.mult),
                   reads=[tbb, bf("cst")], writes=[bf("mixinT")])
            pe_transposes(4, lambda j: gn[:ch, j * 128:(j + 1) * 128], ch, 128, [bf("gn")], ev_g)
            op(DVE, lambda e: e.tensor_tensor(out=xw_tok[:ch, :].rearrange("p (h x) -> p h x", h=8), in0=xsb_tok[:ch, 0:4, :].rearrange("p a (b x) -> p (a b) x", b=2),
                                              in1=wk2[:ch, :].unsqueeze(2).to_broadcast([ch, 8, 64]), op=ALU.mult),
               reads=[XSB, WK2], writes=[bf("xw_tok")])
            bX, bXb = nbank()
            for g in range(2):
                op(PE, lambda e, g=g, bX=bX: e.matmul(bX[:, g * 256:(g + 1) * 256], lhsT=xsb_tok[:ch, 4 + g, :], rhs=xw_tok[:ch, g * 256:(g + 1) * 256], start=True, stop=True),
                   reads=[XSB, bf("xw_tok")], writes=[bXb], sig=(g == 1))
            op(DVE, lambda e: e.tensor_tensor(out=stT[:, :].rearrange("p (h x) -> p h x", h=8), in0=stT[:, :].rearrange("p (h x) -> p h x", h=8),
                                              in1=ck[:, :].unsqueeze(2).to_broadcast([128, 8, 64]), op=ALU.mult),
               reads=[bf("stT"), CKB], writes=[bf("stT")])
            op(DVE, lambda e, bX=bX: e.tensor_tensor(out=stT[:, :], in0=stT[:, :], in1=bX[:, :], op=ALU.add), reads=[bf("stT"), bXb], writes=[bf("stT")])
            op(ACT, lambda e: e.activation(out=stB[:, :], in_=stT[:, :], func=AF.Copy), reads=[bf("stT")], writes=[bf("stB")])


        ssd_front(0)
        for c in range(nb):
            if c + 1 < nb:
                ssd_front(c + 1)
            ssd_back(c)

        if STAGE < 3:
            return
        wu = [wload(wsc_d[7 + c], bf("wsc")) for c in (0, 1)]
        for s8 in range(8):
            bk, bb_ = nbank()
            for hf in range(2):
                for k in range(8):
                    op(PE, lambda e, s8=s8, hf=hf, k=k, bk=bk: e.matmul(bk[:J, hf * 256:(hf + 1) * 256], lhsT=xnT[:, k, s8:nt:8],
                                                                        rhs=wu[hf][0][:, k * 256:(k + 1) * 256], start=(k == 0), stop=(k == 7)),
                       reads=[bf("xnT"), wu[hf][1]], writes=[bb_], sig=(k == 7 and hf == 1))
            op(ACT, lambda e, s8=s8, bk=bk: e.activation(out=u_tok2[:J, :, s8, :], in_=bk[:J, :].rearrange("p (g c) -> p g c", g=32), func=AF.Copy),
               reads=[bb_], writes=[bf("u_tok2")])

        def ev_u(s0, cnt, view, tbb):
            op(ACT, lambda e: e.activation(out=U8[:, s0:s0 + cnt, :J], in_=view, func=AF.Copy), reads=[tbb], writes=[bf("U8")])
        pe_transposes(32, lambda g: u_tok2[:J, g, :, :].rearrange("p s c -> p (s c)"), J, 128, [bf("u_tok2")], ev_u)
        for hf in range(2):
            bSr = [nbank(), nbank()]
            for ri in range(2):
                for gpl in range(8):
                    gp = hf * 8 + gpl
                    for gl in range(2):
                        gg = 2 * gp + gl
                        op(PE, lambda e, bSr=bSr, ri=ri, gpl=gpl, gl=gl, gg=gg: e.matmul(bSr[ri][0][:, gpl * J:(gpl + 1) * J], lhsT=ZT[:, ri, gg, :], rhs=U8[:, gg, :J], start=(gl == 0), stop=(gl == 1)),
                           reads=[bf("ZT"), bf("U8")], writes=[bSr[ri][1]], sig=(gpl == 7 and gl == 1))
            gps = slice(hf * 8, hf * 8 + 8)
            Sre = bSr[0][0][:, :8 * J].rearrange("p (g j) -> p g j", g=8)
            Sim = bSr[1][0][:, :8 * J].rearrange("p (g j) -> p g j", g=8)
            RB = bf("R_")
            rd = [bSr[0][1], bSr[1][1]] + S5C
            op(DVE, lambda e, Sre=Sre, gps=gps: e.tensor_tensor(out=t_a[:, :, :J], in0=Sre, in1=cosT[:, gps, :J], op=ALU.mult), reads=rd, writes=[bf("t_a")])
            op(DVE, lambda e, Sim=Sim, gps=gps: e.tensor_tensor(out=t_b[:, :, :J], in0=Sim, in1=sinT[:, gps, :J], op=ALU.mult), reads=rd, writes=[bf("t_b")])
            op(DVE, lambda e: e.tensor_tensor(out=R_[:, 0, :, :J], in0=t_a[:, :, :J], in1=t_b[:, :, :J], op=ALU.add), reads=[bf("t_a"), bf("t_b")], writes=[RB])
            op(DVE, lambda e, Sim=Sim, gps=gps: e.tensor_tensor(out=t_a[:, :, :J], in0=Sim, in1=cosT[:, gps, :J], op=ALU.mult), reads=rd, writes=[bf("t_a")])
            op(DVE, lambda e, Sre=Sre, gps=gps: e.tensor_tensor(out=t_b[:, :, :J], in0=Sre, in1=sinT[:, gps, :J], op=ALU.mult), reads=rd, writes=[bf("t_b")])
            op(DVE, lambda e: e.tensor_tensor(out=R_[:, 1, :, :J], in0=t_a[:, :, :J], in1=t_b[:, :, :J], op=ALU.subtract), reads=[bf("t_a"), bf("t_b")], writes=[RB])
            for ri in range(2):
                op(DVE, lambda e, ri=ri, gps=gps: e.tensor_tensor(out=t_a[:, :, 0], in0=Hc[:, ri, gps], in1=r8[:, gps], op=ALU.mult), reads=[bf("Hc")] + S5C, writes=[bf("t_a")])
                op(DVE, lambda e, ri=ri: e.tensor_tensor(out=R_[:, ri, :, 0], in0=R_[:, ri, :, 0], in1=t_a[:, :, 0], op=ALU.add), reads=[bf("t_a"), RB], writes=[RB])
            for ri in range(2):
                if J == JMAX:
                    op(DVE, lambda e, ri=ri, gps=gps: e.tensor_tensor_scan(out=G_[:, ri, :, :].rearrange("p g j -> p (g j)"), data0=r8z[:, gps, :].rearrange("p g j -> p (g j)"),
                                                                           data1=R_[:, ri, :, :].rearrange("p g j -> p (g j)"), initial=0.0, op0=ALU.mult, op1=ALU.add),
                       reads=[RB] + S5C, writes=[bf("G_")])
                else:
                    for gpl in range(8):
                        op(DVE, lambda e, ri=ri, gpl=gpl, hf=hf: e.tensor_tensor_scan(out=G_[:, ri, gpl, :J], data0=r8z[:, hf * 8 + gpl, :J], data1=R_[:, ri, gpl, :J], initial=0.0,
                                                                                      op0=ALU.mult, op1=ALU.add),
                           reads=[RB] + S5C, writes=[bf("G_")])
            if DBG and first and seq == "p" and hf == 0:
                op(DVE, lambda e, bSr=bSr, Sre=Sre: e.tensor_copy(out=sgt[:, :8 * J].rearrange("p (g j) -> p g j", g=8), in_=Sre), reads=[bSr[0][1]], writes=[bf("sgt")])
                d_ = dout("dbg_S0", [128, 8, JMAX])
                op(POOL, lambda e, d_=d_: e.dma_start(out=d_, in_=sgt[:, :8 * J].rearrange("p (g j) -> p g j", g=8)), reads=[bf("sgt")], writes=[bf("dbgout")], dma="dbg_S0")
            GB = bf("G_")
            HB = bf("Hn")
            op(DVE, lambda e, gps=gps: e.tensor_tensor(out=t_a[:, :, :J], in0=G_[:, 0, :, :J], in1=cosT[:, gps, :J], op=ALU.mult), reads=[GB] + S5C, writes=[bf("t_a")])
            op(DVE, lambda e, gps=gps: e.tensor_tensor(out=t_b[:, :, :J], in0=G_[:, 1, :, :J], in1=sinT[:, gps, :J], op=ALU.mult), reads=[GB] + S5C, writes=[bf("t_b")])
            op(DVE, lambda e, gps=gps: e.tensor_tensor(out=Hn[:, 0, gps, :J], in0=t_a[:, :, :J], in1=t_b[:, :, :J], op=ALU.subtract), reads=[bf("t_a"), bf("t_b")], writes=[HB])
            op(DVE, lambda e, gps=gps: e.tensor_tensor(out=t_a[:, :, :J], in0=G_[:, 1, :, :J], in1=cosT[:, gps, :J], op=ALU.mult), reads=[GB] + S5C, writes=[bf("t_a")])
            op(DVE, lambda e, gps=gps: e.tensor_tensor(out=t_b[:, :, :J], in0=G_[:, 0, :, :J], in1=sinT[:, gps, :J], op=ALU.mult), reads=[GB] + S5C, writes=[bf("t_b")])
            op(DVE, lambda e, gps=gps: e.tensor_tensor(out=Hn[:, 1, gps, :J], in0=t_a[:, :, :J], in1=t_b[:, :, :J], op=ALU.add), reads=[bf("t_a"), bf("t_b")], writes=[HB])
            if DBG and first and seq == "p" and hf == 0:
                for nm, ap, shp, rd in (("R0", R_[:], [128, 2, 8, JMAX], [bf("R_")]), ("G0", G_[:], [128, 2, 8, JMAX], [bf("G_")]), ("Hn0", Hn[:], [128, 2, 16, JMAX], [bf("Hn")])):
                    d_ = dout("dbg_" + nm, shp)
                    op(DVE, lambda e, d_=d_, ap=ap: e.dma_start(out=d_, in_=ap), reads=rd, writes=[bf("dbgout")], dma="dbg_" + nm)
        op(ACT, lambda e: e.activation(out=Hprev[:, :, :, 0], in_=Hc[:, :, :], func=AF.Copy), reads=[bf("Hc")], writes=[bf("Hprev")])
        op(ACT, lambda e: e.activation(out=Hprev[:, :, :, 1:J], in_=Hn[:, :, :, 0:J - 1], func=AF.Copy), reads=[bf("Hn")], writes=[bf("Hprev")])
        op(ACT, lambda e: e.activation(out=Hc[:, :, :], in_=Hn[:, :, :, J - 1], func=AF.Copy), reads=[bf("Hn"), bf("Hprev")], writes=[bf("Hc")])
        for g0 in range(0, 32, 4):
            bk, bb_ = nbank()
            for gi in range(4):
                gg = g0 + gi
                gp = gg // 2
                osl = slice(gi * 128, (gi + 1) * 128)
                op(PE, lambda e, gg=gg, bk=bk, osl=osl: e.matmul(bk[:J, osl], lhsT=U8[:, gg, :J], rhs=Km[:, gg, :], start=True, stop=False),
                   reads=[bf("Km"), bf("U8")], writes=[bb_], sig=False)
                op(PE, lambda e, gg=gg, gp=gp, bk=bk, osl=osl: e.matmul(bk[:J, osl], lhsT=Hprev[:, 0, gp, :J], rhs=Fg[:, 0, gg, :], start=False, stop=False),
                   reads=[bf("Fm"), bf("Hprev")], writes=[bb_], sig=False)
                op(PE, lambda e, gg=gg, gp=gp, bk=bk, osl=osl: e.matmul(bk[:J, osl], lhsT=Hprev[:, 1, gp, :J], rhs=Fg[:, 1, gg, :], start=False, stop=True),
                   reads=[bf("Fm"), bf("Hprev")], writes=[bb_], sig=(gi == 3))
            op(ACT, lambda e, g0=g0, bk=bk: e.activation(out=yj[:J, :, g0 * 16:(g0 + 4) * 16].rearrange("p t (g c) -> p g t c", g=4),
                                                         in_=bk[:J, :].rearrange("p (g t c) -> p g t c", g=4, t=8), func=AF.Copy),
               reads=[bb_], writes=[bf("yj")])
        for blk in range(4):
            def ev_t(s0, cnt, view, tbb, blk=blk):
                op(ACT, lambda e: e.activation(out=ygT[:, blk, :nt].rearrange("p (j t) -> p t j", t=8)[:, s0:s0 + cnt, :], in_=view, func=AF.Gelu_apprx_tanh),
                   reads=[tbb], writes=[bf("ygT")])
            pe_transposes(8, lambda t8, blk=blk: yj[:J, t8, blk * 128:(blk + 1) * 128], J, 128, [bf("yj")], ev_t)
        for blk in range(4):
            bA, bAb = nbank()
            bG, bGb = nbank()
            op(PE, lambda e, blk=blk, bA=bA: e.matmul(bA[:, :nt], lhsT=GWb[:, blk, 0, :], rhs=ygT[:, blk, :nt], start=True, stop=True), reads=[bf("GWb"), bf("ygT")], writes=[bAb])
            op(PE, lambda e, blk=blk, bG=bG: e.matmul(bG[:, :nt], lhsT=GWb[:, blk, 1, :], rhs=ygT[:, blk, :nt], start=True, stop=True), reads=[bf("GWb"), bf("ygT")], writes=[bGb])
            op(ACT, lambda e, blk=blk, bG=bG: e.activation(out=sgt[:, :nt], in_=bG[:, :nt], func=AF.Sigmoid, bias=C("gbb")[:, blk:blk + 1]), reads=[bGb, bf("cst")], writes=[bf("sgt")])
            op(DVE, lambda e, blk=blk, bA=bA: e.scalar_tensor_tensor(out=mixinT[:, 4 + blk, :nt], in0=bA[:, :nt], scalar=C("gba")[:, blk:blk + 1], in1=sgt[:, :nt], op0=ALU.add, op1=ALU.mult),
               reads=[bAb, bf("sgt"), bf("cst")], writes=[bf("mixinT")])

        if STAGE < 4:
            return
        def norm_res_all(bankmap, wbc_name):
            blocks = sorted(bankmap)
            for b in blocks:
                bk0, bb0, bk1, bb1 = bankmap[b]
                op(ACT, lambda e, b=b, bk0=bk0: e.activation(out=junk[:ch, 0:512], in_=bk0[:ch, :], func=AF.Square, accum_out=stt[:ch, b, 4:5]), reads=[bb0], writes=[bf("junk"), bf("stt")])
                op(ACT, lambda e, b=b, bk1=bk1: e.activation(out=junk[:ch, 512:1024], in_=bk1[:ch, :], func=AF.Square, accum_out=stt[:ch, b, 5:6]), reads=[bb1], writes=[bf("junk"), bf("stt")])
            for b in blocks:
                op(DVE, lambda e, b=b: e.tensor_tensor(out=stt[:ch, b, 0:1], in0=stt[:ch, b, 4:5], in1=stt[:ch, b, 5:6], op=ALU.add), reads=[bf("stt")], writes=[bf("stt")])
            stage_rstd(blocks, ch)
            wbc = C(wbc_name)
            for b in blocks:
                bk0, bb0, bk1, bb1 = bankmap[b]
                mt = mixtmps[b % 2]
                MT = bf("mixtmp%d" % (b % 2))
                op(DVE, lambda e, b=b, bk0=bk0, mt=mt: e.scalar_tensor_tensor(out=mt[:ch, 0:512], in0=bk0[:ch, :], scalar=stt[:ch, b, 3:4], in1=wbc[:ch, 0:512], op0=ALU.mult, op1=ALU.mult),
                   reads=[bb0, bf("stt"), bf("cst")], writes=[MT])
                op(DVE, lambda e, b=b, bk1=bk1, mt=mt: e.scalar_tensor_tensor(out=mt[:ch, 512:1024], in0=bk1[:ch, :], scalar=stt[:ch, b, 3:4], in1=wbc[:ch, 512:1024], op0=ALU.mult, op1=ALU.mult),
                   reads=[bb1, bf("stt"), bf("cst")], writes=[MT])
                op(DVE, lambda e, b=b, mt=mt: e.tensor_tensor(out=xres[:ch, b, :], in0=xres[:ch, b, :], in1=mt[:ch, :], op=ALU.add), reads=[XR[b], MT], writes=[XR[b]])

        allb = [(pb[i], b_pb[i]) for i in range(NB)] + [(pt[i], b_pt[i]) for i in range(2)]
        bankmap = {}
        for b in range(nb):
            bk0, bb0 = allb[2 * b]
            bk1, bb1 = allb[2 * b + 1]
            bankmap[b] = (bk0, bb0, bk1, bb1)
            for hf, (bk, bb_) in enumerate(((bk0, bb0), (bk1, bb1))):
                for k in range(8):
                    op(PE, lambda e, hf=hf, k=k, bk=bk, b=b: e.matmul(bk[:ch, :], lhsT=mixinT[:, k, b * 128:b * 128 + ch], rhs=wout_sb[:, k, hf * 512:(hf + 1) * 512], start=(k == 0), stop=(k == 7)),
                       reads=[bf("mixinT"), bf("wout_sb")], writes=[bb_], sig=(k == 7))
        norm_res_all(bankmap, "wpost_bc")
        norm_transpose_all(list(range(nb)), ch, xnT, bf("xnT"), "wffn_k")

        if STAGE < 5:
            return
        for m in range(22):
            wr, wrb = wload(wsc_d[NWIN + m], bf("wsc"))
            pre = []
            for j in range(2):
                blk = 2 * m + j
                bk, bb_ = nbank()
                for k in range(8):
                    op(PE, lambda e, k=k, j=j, bk=bk, wr=wr: e.matmul(bk[:, :nt], lhsT=wr[:, k * 256 + j * 128:k * 256 + (j + 1) * 128], rhs=xnT[:, k, :nt], start=(k == 0), stop=(k == 7)),
                       reads=[bf("xnT"), wrb], writes=[bb_], sig=(k == 7))
                ui = (2 * m + j) % 4
                UB = bf("upb%d" % ui)
                fwv = C("fw")
                op(POOL, lambda e, blk=blk, ui=ui: e.tensor_copy(out=upb[ui][:, 0:2], in_=fh[:, blk, :]), reads=[bf("fh")], writes=[UB])
                op(ACT, lambda e, ui=ui, bk=bk: e.activation(out=upb[ui][:, 2:2 + nt], in_=bk[:, :nt], func=AF.Copy), reads=[bb_], writes=[UB])
                op(POOL, lambda e, blk=blk, ui=ui: e.tensor_copy(out=fh[:, blk, :], in_=upb[ui][:, nt:nt + 2]), reads=[UB], writes=[bf("fh")])
                FA, FB = bf("fct%d" % ui), bf("fctb%d" % ui)
                op(ACT, lambda e, blk=blk, ui=ui: e.activation(out=fct[ui][:, :nt], in_=upb[ui][:, 0:nt], func=AF.Identity, scale=fwv[:, blk, 0:1], bias=C("fb")[:, blk:blk + 1]),
                   reads=[UB, bf("cst")], writes=[FA])
                op(DVE, lambda e, blk=blk, ui=ui: e.scalar_tensor_tensor(out=fct[ui][:, :nt], in0=upb[ui][:, 1:1 + nt], scalar=fwv[:, blk, 1:2], in1=fct[ui][:, :nt], op0=ALU.mult, op1=ALU.add),
                   reads=[UB, FA], writes=[FA])
                op(DVE, lambda e, blk=blk, ui=ui, bk=bk: e.scalar_tensor_tensor(out=fct[ui][:, :nt], in0=bk[:, :nt], scalar=fwv[:, blk, 2:3], in1=fct[ui][:, :nt], op0=ALU.mult, op1=ALU.add),
                   reads=[bb_, FA], writes=[FA])
                pre.append((ui, FA))
            op(ACT, lambda e, ui=pre[0][0]: e.activation(out=gel[:, :nt], in_=fct[ui][:, :nt], func=AF.Gelu_apprx_tanh), reads=[pre[0][1]], writes=[bf("gel")])
            op(DVE, lambda e, m=m, ui=pre[1][0]: e.tensor_tensor(out=actT[:, m, :nt], in0=gel[:, :nt], in1=fct[ui][:, :nt], op=ALU.mult), reads=[bf("gel"), pre[1][1]], writes=[bf("actT")])
        if SUB < 2:
            return
        for b0 in range(0, nb, 2):
            bs = list(range(b0, min(nb, b0 + 2)))
            banks = {(b, hf): nbank() for b in bs for hf in range(2)}
            for pr in range(11):
                wr, wrb = wload(wdsc_d[pr], bf("wdsc"))
                for mm in range(2):
                    m = 2 * pr + mm
                    for b in bs:
                        for hf in range(2):
                            bk, bb_ = banks[(b, hf)]
                            op(PE, lambda e, m=m, mm=mm, b=b, hf=hf, bk=bk, wr=wr: e.matmul(bk[:ch, :], lhsT=actT[:, m, b * 128:b * 128 + ch], rhs=wr[:, mm * 1024 + hf * 512:mm * 1024 + (hf + 1) * 512],
                                                                                            start=(m == 0), stop=(m == 21)),
                               reads=[bf("actT"), wrb], writes=[bb_], sig=(mm == 1 and b == bs[-1] and hf == 1))
            norm_res_all({b: (banks[(b, 0)][0], banks[(b, 0)][1], banks[(b, 1)][0], banks[(b, 1)][1]) for b in bs}, "wpf_bc")
        if SUB < 3:
            return
        if nt >= 128:
            for b in range(nb):
                op(POOL, lambda e, b=b: e.dma_start(out=ydst[b * 128:(b + 1) * 128, :], in_=xres[:, b, :]), reads=[XR[b]], writes=[bf("yout%d" % b)], dma=("st_y", b))
        else:
            op(POOL, lambda e: e.dma_start(out=ydst, in_=xres[:nt, 0, :]), reads=XR, writes=[bf("yout0")], dma=("st_y", 0))

    def init_state(seq):
        if seq == "p":
            op(POOL, lambda e: e.memset(xh[:, :, :], 0.0), writes=[bf("xh")])
            op(POOL, lambda e: e.memset(stT[:, :], 0.0), writes=[bf("stT")])
            op(POOL, lambda e: e.memset(stB[:, :], 0.0), writes=[bf("stB")])
            op(POOL, lambda e: e.memset(Hc[:, :, :], 0.0), writes=[bf("Hc")])
            op(POOL, lambda e: e.memset(fh[:, :, :], 0.0), writes=[bf("fh")])
        else:
            op(SP, lambda e: e.dma_start(out=xh[:, :, :], in_=cconv_d), writes=[bf("xh")], dma="ld_st0")
            op(SP, lambda e: e.dma_start(out=Hc[:, 0, :], in_=cs5re_d), writes=[bf("Hc")], dma="ld_st1")
            op(SP, lambda e: e.dma_start(out=Hc[:, 1, :], in_=cs5im_d), writes=[bf("Hc")], dma="ld_st1")
            op(SP, lambda e: e.dma_start(out=fh[:, :, :], in_=cffn_d), writes=[bf("fh")], dma="ld_st3")
            op(SP, lambda e: e.dma_start(out=mixtmp[:, 0:512].rearrange("p (a n) -> p a n", a=4), in_=cssd_d), writes=[bf("mixtmp0")], dma="ld_st4")
            bk, bb_ = nbank()
            for j in range(4):
                op(PE, lambda e, j=j, bk=bk: e.transpose(out=bk[:, j * 128:(j + 1) * 128], in_=mixtmp[:, j * 128:(j + 1) * 128], identity=identF),
                   reads=[bf("mixtmp0"), bf("cst")], writes=[bb_], sig=(j == 3))
            op(DVE, lambda e, bk=bk: e.tensor_copy(out=stT[:, :], in_=bk[:, :]), reads=[bb_], writes=[bf("stT")])
            op(ACT, lambda e: e.activation(out=stB[:, :], in_=stT[:, :], func=AF.Copy), reads=[bf("stT")], writes=[bf("stB")])

    def store_state(sfx):
        op(SP, lambda e: e.dma_start(out=outs["conv" + sfx], in_=xh[:, :, :]), reads=[bf("xh")], writes=[bf("sout0" + sfx)], dma="st_s0" + sfx)
        op(SP, lambda e: e.dma_start(out=outs["s5re" + sfx], in_=Hc[:, 0, :]), reads=[bf("Hc")], writes=[bf("sout1" + sfx)], dma="st_s1" + sfx)
        op(SP, lambda e: e.dma_start(out=outs["s5im" + sfx], in_=Hc[:, 1, :]), reads=[bf("Hc")], writes=[bf("sout2" + sfx)], dma="st_s2" + sfx)
        op(SP, lambda e: e.dma_start(out=outs["ffn" + sfx], in_=fh[:, :, :]), reads=[bf("fh")], writes=[bf("sout3" + sfx)], dma="st_s3" + sfx)
        bk, bb_ = nbank()
        for j in range(4):
            op(PE, lambda e, j=j, bk=bk: e.transpose(out=bk[:, j * 128:(j + 1) * 128], in_=stT[:, j * 128:(j + 1) * 128], identity=identF),
               reads=[bf("stT"), bf("cst")], writes=[bb_], sig=(j == 3))
        op(DVE, lambda e, bk=bk: e.tensor_copy(out=mixtmp[:, 0:512], in_=bk[:, :]), reads=[bb_], writes=[bf("mixtmp0")])
        op(SP, lambda e: e.dma_start(out=outs["ssd" + sfx], in_=mixtmp[:, 0:512].rearrange("p (a n) -> p a n", a=4)), reads=[bf("mixtmp0")], writes=[bf("sout4" + sfx)], dma="st_s4" + sfx)

    init_state("p")
    for t0 in range(0, T, NT):
        run_tile(x_d[t0:t0 + NT, :], y_d[t0:t0 + NT, :], NT, t0 == 0, "p")
    store_state("p")
    init_state("s")
    run_tile(xs_d, ys_d, T_SAMPLE, True, "s")
    store_state("s")
    fin = [bf("yout%d" % b) for b in range(4)] + [bf("sout%d%s" % (i, sfx)) for i in range(5) for sfx in ("p", "s")]
    op(SP, lambda e: e.nop(), reads=fin, sig=False)
    op(POOL, lambda e: e.nop(), reads=fin, sig=False)
    P.emit()
    return nc


_CACHE = {}


def run(inputs, T=T_PROMPT):
    shared, pk, pk2 = host_shared(inputs)
    key = (T, pk.n)
    if key not in _CACHE:
        _CACHE[key] = build(T, pk, pk2)
    nc = _CACHE[key]
    in_maps = []
    for i in range(NCORES):
        d = host_core(inputs, i, T)
        d.update(shared)
        in_maps.append(d)
    res = run_bass_kernel_spmd(nc, in_maps, core_ids=list(range(NCORES)))
    R = res.results
    y = np.stack([R[i]["y"] for i in range(NCORES)])
    ys = np.stack([R[i]["ys"] for i in range(NCORES)])

    def conv(sfx):
        return np.stack([R[i]["o_conv" + sfx].transpose(2, 1, 0).reshape(3, 1024) for i in range(NCORES)])[None]

    def ssd(sfx):
        return np.stack([R[i]["o_ssd" + sfx].transpose(1, 0, 2).reshape(8, 64, 128) for i in range(NCORES)])[None]

    def s5(nm, sfx):
        return np.stack([unpg(R[i]["o_" + nm + sfx]) for i in range(NCORES)])[None]

    def ffn(sfx):
        out = []
        for i in range(NCORES):
            a = R[i]["o_ffn" + sfx].transpose(2, 1, 0).reshape(2, 5632)
            full = np.zeros((2, 5632), np.float32)
            full[:, FFN_COLS] = a
            out.append(full)
        return np.stack(out)[None]
    f = lambda a: np.ascontiguousarray(a, dtype=np.float32)
    return (f(y), f(ys),
            f(conv("p")), f(ssd("p")), f(s5("s5re", "p")), f(s5("s5im", "p")), f(ffn("p")),
            f(conv("s")), f(ssd("s")), f(s5("s5re", "s")), f(s5("s5im", "s")), f(ffn("s")))


def kernel(**inputs):
    return run(inputs, T_PROMPT)
```
